# Optimizing a Trainium2 kernel written in Bass

```python
import math
import jax, jax.numpy as jnp
from jax import lax
import numpy as np

D_MODEL = 1024
BATCH = 4
SEQ = 4096
DEPTH = 1

MEM_LEN = 256
HEAD_DIM = 64
N_ATTN_HEADS = 8
ATTN_WIDTH = N_ATTN_HEADS * HEAD_DIM
CONV_WIDTH = D_MODEL - ATTN_WIDTH
N_CONV_GROUPS = CONV_WIDTH // HEAD_DIM
MIX_WIDTH = ATTN_WIDTH + CONV_WIDTH
IN_PROJ_COLS = 3 * ATTN_WIDTH + 3 * CONV_WIDTH
DILATED_PATTERNS = ((128, 1), (512, 4), (2048, 16))
SEQ_PAD_MULT = max(w for w, _ in DILATED_PATTERNS)
N_BUCKETS = 32
BUCKET_MAX_EXACT = N_BUCKETS // 2
BUCKET_MAX_DISTANCE = 2048
SHORT_CONV_K = 3
FFN_CONV_K = 3
D_FF = 2816
N_MEM_HEADS = 4
MEM_HEAD_DIM = D_MODEL // N_MEM_HEADS
EPS = 1e-6

kernel_name = "hymba_dilated_shortconv_convffn_memxattn"


def rms_norm(x, g):
    xf = x.astype(jnp.float32)
    y = xf * lax.rsqrt(jnp.mean(xf * xf, axis=-1, keepdims=True) + EPS)
    return (y * g.astype(jnp.float32)).astype(x.dtype)


def causal_dwconv(u, w):
    k_width = w.shape[0]
    s = u.shape[1]
    up = jnp.pad(u, ((0, 0), (k_width - 1, 0), (0, 0)))
    out = up[:, 0:s, :] * w[0]
    for k in range(1, k_width):
        out = out + up[:, k:k + s, :] * w[k]
    return out


def t5_bucket(distance):
    d = jnp.maximum(distance, 1).astype(jnp.float32)
    large = BUCKET_MAX_EXACT + (
        jnp.log(d / BUCKET_MAX_EXACT) / math.log(BUCKET_MAX_DISTANCE / BUCKET_MAX_EXACT)
        * (N_BUCKETS - BUCKET_MAX_EXACT)).astype(jnp.int32)
    large = jnp.minimum(large, N_BUCKETS - 1)
    return jnp.where(distance < BUCKET_MAX_EXACT, distance, large)


def dilated_window_attention(q, k, v, rel_bias, window, dilation):
    b, h, sp, dh = q.shape
    w = window // dilation
    n_sub = sp // dilation
    nb = n_sub // w

    def to_blocks(t):
        t = t.reshape(b, h, n_sub, dilation, -1).transpose(0, 1, 3, 2, 4)
        return t.reshape(b, h, dilation, nb, w, t.shape[-1])

    def from_blocks(t):
        t = t.reshape(b, h, dilation, n_sub, -1).transpose(0, 1, 3, 2, 4)
        return t.reshape(b, h, sp, t.shape[-1])

    def with_prev(t):
        prev = jnp.pad(t, ((0, 0), (0, 0), (0, 0), (1, 0), (0, 0), (0, 0)))[:, :, :, :-1]
        return jnp.concatenate([prev, t], axis=4)

    qb = to_blocks(q)
    kk = with_prev(to_blocks(k))
    vv = with_prev(to_blocks(v))

    logits = jnp.einsum('bhrnid,bhrnjd->bhrnij', qb, kk).astype(jnp.float32)
    qi = jnp.arange(w)[:, None]
    kj = jnp.arange(2 * w)[None, :]
    steps = qi + w - kj
    valid_local = (steps >= 0) & (steps <= w)
    block_idx = jnp.arange(nb)[:, None, None]
    valid = valid_local[None] & ((block_idx > 0) | (kj >= w)[None])
    bucket = t5_bucket(jnp.clip(steps, 0, w) * dilation)
    bias = rel_bias.astype(jnp.float32)[:, bucket]
    logits = logits + bias[None, :, None, None]
    logits = jnp.where(valid, logits, -jnp.inf)
    m = jnp.max(logits, axis=-1, keepdims=True)
    p = jnp.exp(logits - m)
    s = jnp.sum(p, axis=-1, keepdims=True)
    o = jnp.einsum('bhrnij,bhrnjd->bhrnid', p, vv.astype(jnp.float32)) / s
    return from_blocks(o), from_blocks(m), from_blocks(s)


def hybrid_mixer(h, rel_bias, w_in, w_short_conv, g_attn_out, g_conv_out, w_out):
    b, s, _ = h.shape
    proj = h @ w_in
    q, k, v, gate_b, gate_c, x_in = jnp.split(
        proj, [ATTN_WIDTH, 2 * ATTN_WIDTH, 3 * ATTN_WIDTH,
               3 * ATTN_WIDTH + CONV_WIDTH, 3 * ATTN_WIDTH + 2 * CONV_WIDTH], axis=-1)

    sp = ((s + SEQ_PAD_MULT - 1) // SEQ_PAD_MULT) * SEQ_PAD_MULT

    def heads(t):
        t = t.reshape(b, s, N_ATTN_HEADS, HEAD_DIM).transpose(0, 2, 1, 3)
        return jnp.pad(t, ((0, 0), (0, 0), (0, sp - s), (0, 0)))

    qh = heads(q) * (HEAD_DIM ** -0.5)
    kh, vh = heads(k), heads(v)
    branches = [dilated_window_attention(qh, kh, vh, rel_bias, w, d) for (w, d) in DILATED_PATTERNS]
    m_all = branches[0][1]
    for _, m_i, _ in branches[1:]:
        m_all = jnp.maximum(m_all, m_i)
    num = jnp.zeros_like(branches[0][0])
    den = jnp.zeros_like(m_all)
    for o_i, m_i, s_i in branches:
        wt = s_i * jnp.exp(m_i - m_all)
        num = num + wt * o_i
        den = den + wt
    attn = (num / den)[:, :, :s].transpose(0, 2, 1, 3).reshape(b, s, ATTN_WIDTH).astype(h.dtype)

    conv = gate_b * causal_dwconv(gate_c * x_in, w_short_conv)

    mixed = jnp.concatenate([rms_norm(attn, g_attn_out), rms_norm(conv, g_conv_out)], axis=-1)
    return mixed @ w_out


def memory_cross_attention(h, mem_n, w_xq, w_xk, w_xv, w_xo):
    b, s, _ = h.shape
    q = (h @ w_xq).reshape(b, s, N_MEM_HEADS, MEM_HEAD_DIM)
    k = (mem_n @ w_xk).reshape(b, MEM_LEN, N_MEM_HEADS, MEM_HEAD_DIM)
    v = (mem_n @ w_xv).reshape(b, MEM_LEN, N_MEM_HEADS, MEM_HEAD_DIM)
    logits = jnp.einsum('bshd,bmhd->bhsm', q, k).astype(jnp.float32) * (MEM_HEAD_DIM ** -0.5)
    p = jax.nn.softmax(logits, axis=-1)
    o = jnp.einsum('bhsm,bmhd->bshd', p, v.astype(jnp.float32)).astype(h.dtype)
    return o.reshape(b, s, D_MODEL) @ w_xo


def conv_ffn(h, w_up, w_ffn_conv, b_ffn_conv, w_down):
    up = causal_dwconv(h @ w_up, w_ffn_conv) + b_ffn_conv
    gate, val = jnp.split(up, 2, axis=-1)
    return (jax.nn.silu(gate) * val) @ w_down


def setup_inputs(seed: int = 0) -> dict:
    key = jax.random.key(seed)
    ks = iter(jax.random.split(key, 32))
    f32 = jnp.float32
    L = DEPTH

    def dense(shape, fan_in):
        return jax.random.normal(next(ks), shape, f32) * fan_in ** -0.5

    def gain(shape):
        return 1.0 + 0.02 * jax.random.normal(next(ks), shape, f32)

    return {
        "x": jax.random.normal(next(ks), (BATCH, SEQ, D_MODEL), f32),
        "mem": jax.random.normal(next(ks), (BATCH, MEM_LEN, D_MODEL), f32),
        "rel_bias": 0.2 * jax.random.normal(next(ks), (N_ATTN_HEADS, N_BUCKETS), f32),
        "g_mix": gain((L, D_MODEL)),
        "w_in": dense((L, D_MODEL, IN_PROJ_COLS), D_MODEL),
        "w_short_conv": dense((L, SHORT_CONV_K, CONV_WIDTH), SHORT_CONV_K),
        "g_attn_out": gain((L, ATTN_WIDTH)),
        "g_conv_out": gain((L, CONV_WIDTH)),
        "w_out": dense((L, MIX_WIDTH, D_MODEL), MIX_WIDTH),
        "g_xattn": gain((L, D_MODEL)),
        "g_mem": gain((L, D_MODEL)),
        "w_xq": dense((L, D_MODEL, D_MODEL), D_MODEL),
        "w_xk": dense((L, D_MODEL, D_MODEL), D_MODEL),
        "w_xv": dense((L, D_MODEL, D_MODEL), D_MODEL),
        "w_xo": dense((L, D_MODEL, D_MODEL), D_MODEL),
        "g_ffn": gain((L, D_MODEL)),
        "w_up": dense((L, D_MODEL, 2 * D_FF), D_MODEL),
        "w_ffn_conv": dense((L, FFN_CONV_K, 2 * D_FF), FFN_CONV_K),
        "b_ffn_conv": 0.02 * jax.random.normal(next(ks), (L, 2 * D_FF), f32),
        "w_down": dense((L, D_FF, D_MODEL), D_FF),
        "g_final": gain((D_MODEL,)),
    }


def reference(x, mem, rel_bias, g_mix, w_in, w_short_conv, g_attn_out, g_conv_out, w_out,
              g_xattn, g_mem, w_xq, w_xk, w_xv, w_xo, g_ffn, w_up, w_ffn_conv, b_ffn_conv,
              w_down, g_final):
    for l in range(DEPTH):
        x = x + hybrid_mixer(rms_norm(x, g_mix[l]), rel_bias, w_in[l], w_short_conv[l],
                             g_attn_out[l], g_conv_out[l], w_out[l])
        x = x + memory_cross_attention(rms_norm(x, g_xattn[l]), rms_norm(mem, g_mem[l]),
                                       w_xq[l], w_xk[l], w_xv[l], w_xo[l])
        x = x + conv_ffn(rms_norm(x, g_ffn[l]), w_up[l], w_ffn_conv[l], b_ffn_conv[l], w_down[l])
    return rms_norm(x, g_final)
```

```python
import math
from contextlib import ExitStack
import numpy as np
import concourse.bass as bass
import concourse.mybir as mybir
from concourse.bass_utils import run_bass_kernel_spmd

F32, BF16 = mybir.dt.float32, mybir.dt.bfloat16
AF = mybir.ActivationFunctionType
ALU = mybir.AluOpType

P = 128
E0, NE, EBW, NEB = 2016, 2080, 416, 5
CBW, NCB = 504, 4
F0, FBW, NFB = 30, 410, 5
NPAIR = 22
GROUPS = [5, 4, 5, 4, 4]
NSLOT = 9
EPS = 1e-6
NV = 240
C_GMIX, C_GX, C_GMEM, C_GFFN, C_GFIN, C_GATT, C_GCONV, C_WSC, C_WFC, C_BFC, C_FLAG, C_EPS, C_TINY = \
    0, 8, 16, 24, 32, 40, 44, 48, 60, 192, 236, 237, 238
ARENA_BYTES = 211968
SAME_ENG_SYNC = True
SEM_EPOCH = 30000


class Buf:
    __slots__ = ("name", "ap", "lastw", "readers", "sem", "dcount", "start", "end", "pw")

    def __init__(self, name, ap, start=0, end=0):
        self.name, self.ap, self.start, self.end = name, ap, start, end
        self.lastw, self.readers, self.sem, self.dcount = None, [], None, 0
        self.pw = []

    def v3(self, a):
        return self.ap.rearrange("p (a b) -> p a b", a=a)


class Op:
    __slots__ = ("eng", "fn", "deps", "sig", "val", "dma_buf", "dma_val")


class Sched:
    ENGS = ("pe", "act", "dve", "pool", "sp")

    def __init__(self):
        self.ops = {e: [] for e in self.ENGS}
        self.dma_bufs = []

    def op(self, eng, fn, reads=(), writes=(), dma=None, pwrites=()):
        o = Op()
        o.eng, o.fn, o.sig, o.val, o.dma_buf, o.dma_val = eng, fn, False, 0, None, 0
        deps = set()
        for b in reads:
            if b.lastw is not None:
                deps.add(b.lastw)
            deps.update(b.pw)
        for b in writes:
            if b.lastw is not None:
                deps.add(b.lastw)
            deps.update(b.pw)
            deps.update(b.readers)
        for b in pwrites:
            if b.lastw is not None:
                deps.add(b.lastw)
            deps.update(b.readers)
        keep = []
        for d in deps:
            if d.dma_buf is None and d.eng == eng and (eng == "pe" or not SAME_ENG_SYNC):
                continue
            keep.append(d)
            d.sig = True
        o.deps = keep
        for b in reads:
            b.readers.append(o)
        for b in writes:
            b.lastw = o
            b.readers = []
            b.pw = []
        for b in pwrites:
            b.pw.append(o)
        if dma is not None:
            if dma.dcount == 0:
                self.dma_bufs.append(dma)
            dma.dcount += 1
            o.dma_buf, o.dma_val = dma, 16 * dma.dcount
        self.ops[eng].append(o)
        return o

    def finalize(self):
        self.nepoch = {}
        for e in self.ENGS:
            n = 0
            for o in self.ops[e]:
                if o.dma_buf is None and o.sig:
                    n += 1
                    o.val = n
            self.nepoch[e] = max(1, (n + SEM_EPOCH - 1) // SEM_EPOCH)

    def sigpair(self, o):
        if o.dma_buf is not None:
            return ("d", id(o.dma_buf)), o.dma_val
        ep = (o.val - 1) // SEM_EPOCH
        return (o.eng, ep), o.val - ep * SEM_EPOCH

    def emit(self, eng, engobj, sems, final_waits=()):
        waited = {}
        for o in self.ops[eng]:
            need = {}
            for d in o.deps:
                k, v = self.sigpair(d)
                if v > need.get(k, 0):
                    need[k] = v
            for k, v in need.items():
                if waited.get(k, 0) < v:
                    engobj.wait_ge(sems[k], v)
                    waited[k] = v
            if o.dma_buf is not None and o.dma_val > 16:
                k = ("d", id(o.dma_buf))
                if waited.get(k, 0) < o.dma_val - 16:
                    engobj.wait_ge(sems[k], o.dma_val - 16)
                    waited[k] = o.dma_val - 16
            inst = o.fn(engobj)
            if o.dma_buf is not None:
                inst.then_inc(sems[("d", id(o.dma_buf))], 16)
            elif o.sig:
                k, _ = self.sigpair(o)
                inst.then_inc(sems[k], 1)
        for b in final_waits:
            if b.dcount == 0:
                continue
            engobj.wait_ge(sems[("d", id(b))], 16 * b.dcount)


class Arena:
    def __init__(self, handle, nbytes):
        self.h32 = handle
        self.hbf = handle.bitcast(BF16)
        self.free = [(0, nbytes)]
        self.dead = []
        self.peak = 0
        self.used = 0

    def alloc(self, name, cols, dt):
        esz = 4 if dt == F32 else 2
        nb = (cols * esz + 63) // 64 * 64
        for i, (s, e) in enumerate(self.free):
            if e - s >= nb:
                self.free[i] = (s + nb, e)
                if self.free[i][0] == self.free[i][1]:
                    del self.free[i]
                break
        else:
            raise RuntimeError(f"arena OOM for {name} ({nb} B); free={self.free}")
        if dt == F32:
            ap = self.h32[:, s // 4: s // 4 + cols]
        else:
            ap = self.hbf[:, s // 2: s // 2 + cols]
        b = Buf(name, ap, s, s + nb)
        inh = []
        for (ds, de, db) in self.dead:
            if ds < b.end and b.start < de:
                if db.lastw is not None:
                    inh.append(db.lastw)
                inh.extend(db.pw)
                inh.extend(db.readers)
        b.readers = inh
        self.used += nb
        self.peak = max(self.peak, self.used)
        return b

    def release(self, *bufs):
        for b in bufs:
            self.used -= b.end - b.start
            self.dead.append((b.start, b.end, b))
            self.free.append((b.start, b.end))
        self.free.sort()
        m = []
        for s, e in self.free:
            if m and m[-1][1] == s:
                m[-1] = (m[-1][0], e)
            else:
                m.append((s, e))
        self.free = m


def cols(start, step, count):
    return slice(start, start + step * (count - 1) + 1, step)


class _Stop(Exception):
    pass


def build_program(stop_after=99):
    nc = bass.Bass("TRN2", target_bir_lowering=False)

    def din(name, shape):
        return nc.dram_tensor(name, shape, F32, kind="ExternalInput").ap()

    xT = din("xT", [1024, 4096])
    memT = din("memT", [1024, 256])
    tbh = din("tbh", [128, 24 * 384])
    vecs_d = din("vecs", [128, NV])
    ident_d = din("ident", [128, 128])
    w_in_d = din("w_in", [1024, 3072])
    w_out_d = din("w_out", [1024, 1024])
    w_xq_d = din("w_xq", [1024, 1024])
    w_xk_d = din("w_xk", [1024, 1024])
    w_xv_d = din("w_xv", [1024, 1024])
    w_xo_d = din("w_xo", [1024, 1024])
    w_up_d = din("w_up", [1024, 5632])
    w_down_d = din("w_down", [2816, 1024])
    yT = nc.dram_tensor("yT", [1024, 2048], F32, kind="ExternalOutput").ap()

    def wview(w):
        return w.rearrange("(kc p) n -> p kc n", p=128)

    S = Sched()
    stack = ExitStack()
    arena_h = stack.enter_context(nc.sbuf_tensor("arena", [128, ARENA_BYTES // 4], F32))
    A = Arena(arena_h, ARENA_BYTES)
    psall = stack.enter_context(nc.psum_tensor("psall", [128, 4096], F32))
    psall_bf = psall.bitcast(BF16)

    class PSB:
        def __init__(self, h, base):
            self.h, self.base = h, base

        def __getitem__(self, idx):
            p, c = idx
            return self.h[p, slice(self.base + (c.start or 0), self.base + c.stop, c.step)]

    PS = [(Buf(f"ps{i}", None), PSB(psall, i * 512), PSB(psall_bf, i * 1024)) for i in range(8)]
    psi = [0]

    def nextps():
        r = PS[psi[0] % 8]
        psi[0] += 1
        return r

    outsem = [Buf("outsem0", None), Buf("outsem1", None)]

    def MM(out, lhsT, rhs, start, stop, rd, wr):
        return S.op("pe", lambda e: e.matmul(out, lhsT=lhsT, rhs=rhs, start=start, stop=stop), rd, wr)

    def TR(out, in_, ident, rd, wr):
        return S.op("pe", lambda e: e.transpose(out, in_, ident), rd, wr)

    def ACT(out, in_, func, rd, wr, scale=None, bias=None, pw=()):
        kw = {}
        if scale is not None:
            kw["scale"] = scale
        if bias is not None:
            kw["bias"] = bias
        return S.op("act", lambda e: e.activation(out=out, in_=in_, func=func, **kw), rd, wr, pwrites=pw)

    def TT(out, in0, in1, op, rd, wr, eng="dve", pw=()):
        return S.op(eng, lambda e: e.tensor_tensor(out=out, in0=in0, in1=in1, op=op), rd, wr, pwrites=pw)

    def TS(out, in0, s1, op0, rd, wr, s2=None, op1=None, eng="dve"):
        if op1 is None:
            return S.op(eng, lambda e: e.tensor_scalar(out=out, in0=in0, scalar1=s1, scalar2=None, op0=op0), rd, wr)
        return S.op(eng, lambda e: e.tensor_scalar(out=out, in0=in0, scalar1=s1, scalar2=s2, op0=op0, op1=op1), rd, wr)

    def STT(out, in0, sc, in1, op0, op1, rd, wr, pw=()):
        return S.op("dve", lambda e: e.scalar_tensor_tensor(out=out, in0=in0, scalar=sc, in1=in1, op0=op0, op1=op1), rd, wr,
                    pwrites=pw)

    def CP(out, in_, rd, wr, eng="dve", pw=()):
        if eng == "act":
            return ACT(out, in_, AF.Copy, rd, wr, pw=pw)
        return S.op(eng, lambda e: e.tensor_copy(out=out, in_=in_), rd, wr, pwrites=pw)

    def MSET(ap, val, wr, eng="dve"):
        return S.op(eng, lambda e: e.memset(ap, val), (), wr)

    def DMA(eng, out, in_, rd, wr, dmabuf):
        return S.op(eng, lambda e: e.dma_start(out=out, in_=in_), rd, wr, dma=dmabuf)

    evac_flip = [0]

    def EVAC(out, in_, rd, wr, pw=()):
        evac_flip[0] ^= 1
        return CP(out, in_, rd, wr, eng="act" if evac_flip[0] else "dve", pw=pw)

    def CK(n):
        if n >= stop_after:
            raise _Stop()

    def record():
        vecs = A.alloc("vecs", NV, F32)
        ones = A.alloc("ones", 128, BF16)
        ident = A.alloc("ident", 128, BF16)
        DMA("sp", vecs.ap, vecs_d[:, :], (), [vecs], vecs)
        DMA("pool", ident.ap, ident_d[:, :], (), [ident], ident)
        MSET(ones.ap, 1.0, [ones])

        def vcol(c):
            return vecs.ap[:, c:c + 1]

        def norm_stats(src_ap3, nch, N, dim, srcbufs, sq, lnb, rstd):
            sqv = sq.ap[:, 0:nch * N].rearrange("p (a b) -> p a b", a=nch)
            ACT(sqv, src_ap3, AF.Square, srcbufs, [sq])
            pb, ph, _ = nextps()
            for c in range(nch):
                MM(ph[:, 0:N], ones.ap, sqv[:, c, :], c == 0, c == nch - 1, [ones, sq], [pb])
            ACT(lnb.ap[:, 0:N], ph[:, 0:N], AF.Ln, [pb, vecs], [lnb], scale=1.0 / dim, bias=vcol(C_EPS))
            ACT(rstd.ap[:, 0:N], lnb.ap[:, 0:N], AF.Exp, [lnb], [rstd], scale=-0.5)

        w_in = A.alloc("w_in", 8 * 3072, BF16)
        w_in3 = w_in.v3(8)
        w_in_g = {}

        def load_w_in_group(g, after=()):
            gb = Buf(f"w_in_g{g}", None)
            DMA("pool", w_in3[:, :, g * 512:(g + 1) * 512], wview(w_in_d)[:, :, g * 512:(g + 1) * 512], after, [gb], gb)
            w_in_g[g] = gb

        prev_g = None
        for g_ in (1, 2, 0, 3, 4, 5):
            load_w_in_group(g_, [w_in_g[prev_g]] if prev_g is not None else ())
            prev_g = g_
        kT = A.alloc("kT", 4 * 4096, BF16)
        vT = A.alloc("vT", 4 * 4096, BF16)
        qT = A.alloc("qT", 4 * NE, BF16)
        convT = A.alloc("convT", 4 * NE, BF16)
        kT3, vT3, qT3, convT3 = kT.v3(4), vT.v3(4), qT.v3(4), convT.v3(4)
        xs = A.alloc("xs", 8 * CBW, F32)
        sq = A.alloc("sq", 8 * CBW, BF16)
        hN = [A.alloc(f"hN{i}", 8 * CBW, BF16) for i in range(2)]
        lnb = A.alloc("lnb", CBW, F32)
        rstd = [A.alloc(f"rstd{i}", CBW, F32) for i in range(2)]
        c_sb = [A.alloc(f"c_sb{i}", EBW, F32) for i in range(2)]
        ub = [A.alloc(f"ub{i}", EBW + 2, F32) for i in range(4)]
        t1 = [A.alloc(f"t1_{i}", EBW, F32) for i in range(2)]

        cblk = [(j * CBW, CBW, False) for j in range(NCB)]
        eblk = [(E0 + j * EBW, EBW, True) for j in range(NEB)]
        blocks = []
        for j in range(NEB):
            if j < NCB:
                blocks.append(cblk[j])
            blocks.append(eblk[j])
        first_e = blocks.index(eblk[0])

        def load_norm(j):
            s, N, isE = blocks[j]
            xs3 = xs.ap[:, 0:8 * N].rearrange("p (a b) -> p a b", a=8)
            DMA("sp", xs3, xT.rearrange("(c p) t -> p c t", p=128)[:, :, s:s + N], (), [xs], xs)
            norm_stats(xs3, 8, N, 1024.0, [xs], sq, lnb, rstd[j % 2])
            h3 = hN[j % 2].ap[:, 0:8 * N].rearrange("p (a b) -> p a b", a=8)
            for c in range(8):
                STT(h3[:, c, :], xs3[:, c, :], vcol(C_GMIX + c), rstd[j % 2].ap[:, 0:N], ALU.mult, ALU.mult,
                    [xs, vecs, rstd[j % 2]], [], pw=[hN[j % 2]])

        def proj_group(j, g, oc):
            s, N, isE = blocks[j]
            h3 = hN[j % 2].ap[:, 0:8 * N].rearrange("p (a b) -> p a b", a=8)
            pb, ph, _ = nextps()
            c0 = g * 512 + oc * 128
            for kc in range(8):
                MM(ph[:, 0:N], w_in3[:, kc, c0:c0 + 128], h3[:, kc, :], kc == 0, kc == 7,
                   [w_in_g[g], w_in, hN[j % 2]], [pb])
            return pb, ph

        for i in range(4):
            MSET(ub[i].ap[:, 0:2], 0.0, [ub[i]])

        def proj(j):
            s, N, isE = blocks[j]
            for g, dst, dst3 in ((1, kT, kT3), (2, vT, vT3)):
                for oc in range(4):
                    pb, ph = proj_group(j, g, oc)
                    EVAC(dst3[:, oc, s:s + N], ph[:, 0:N], [pb], [], pw=[dst])
            if not isE:
                return
            e0 = s - E0
            for oc in range(4):
                pb, ph = proj_group(j, 0, oc)
                EVAC(qT3[:, oc, e0:e0 + N], ph[:, 0:N], [pb], [], pw=[qT])
            for oc in range(4):
                pbb, phb = proj_group(j, 3, oc)
                pbc, phc = proj_group(j, 4, oc)
                pbx, phx = proj_group(j, 5, oc)
                cs = c_sb[oc % 2]
                tt = t1[oc % 2]
                u = ub[oc]
                ACT(cs.ap[:, 0:N], phc[:, 0:N], AF.Copy, [pbc], [cs])
                if j > first_e:
                    CP(u.ap[:, 0:2], u.ap[:, N:N + 2], [u], [u])
                TT(u.ap[:, 2:N + 2], phx[:, 0:N], cs.ap[:, 0:N], ALU.mult, [pbx, cs], [u])
                wc = C_WSC + oc * 3
                TS(tt.ap[:, 0:N], u.ap[:, 0:N], vcol(wc), ALU.mult, [u, vecs], [tt])
                STT(tt.ap[:, 0:N], u.ap[:, 1:N + 1], vcol(wc + 1), tt.ap[:, 0:N], ALU.mult, ALU.add, [u, vecs, tt], [tt])
                STT(tt.ap[:, 0:N], u.ap[:, 2:N + 2], vcol(wc + 2), tt.ap[:, 0:N], ALU.mult, ALU.add, [u, vecs, tt], [tt])
                TT(convT3[:, oc, e0:e0 + N], tt.ap[:, 0:N], phb[:, 0:N], ALU.mult, [tt, pbb], [], pw=[convT])

        CK(1)
        load_norm(0)
        proj(0)
        load_norm(1)
        for j in range(1, len(blocks)):
            if j + 1 < len(blocks):
                load_norm(j + 1)
            proj(j)

        CK(2)
        A.release(w_in, xs, sq, hN[0], hN[1], lnb, rstd[0], rstd[1], c_sb[0], c_sb[1], ub[0], ub[1], ub[2], ub[3], t1[0], t1[1])

        Ttp = [A.alloc(f"Ttp{i}", 6 * 384, BF16) for i in range(2)]

        def load_T(hp):
            tb_ = Ttp[hp % 2]
            DMA("pool", tb_.ap, tbh[:, hp * 6 * 384:(hp + 1) * 6 * 384], (), [tb_], tb_)

        def exp_T(hp):
            tb_ = Ttp[hp % 2]
            ACT(tb_.ap, tb_.ap, AF.Exp, [tb_], [tb_])

        load_T(0)
        exp_T(0)
        qm = A.alloc("qm", 2 * NE, BF16)
        qm3 = qm.v3(2)
        MSET(qm3[64:128, 0, :], 0.0, [qm])
        MSET(qm3[0:64, 1, :], 0.0, [qm])
        attnT = A.alloc("attnT", 4 * NE, BF16)
        attnT3 = attnT.v3(4)
        acc = A.alloc("acc", 2 * NE, F32)
        acc3 = acc.v3(2)
        e_sb = [A.alloc(f"e_sb{i}", 512, BF16) for i in range(3)]
        pT = [(A.alloc(f"pTa{i}", 256, BF16), A.alloc(f"pTb{i}", 256, BF16)) for i in range(3)]
        NVS = 8
        vown = [A.alloc(f"vown{i}", 256, BF16) for i in range(NVS)]
        NVC = 4
        vctx = [A.alloc(f"vctx{i}", 256, BF16) for i in range(NVC)]
        rdb = [A.alloc(f"rd{i}", EBW, F32) for i in range(2)]
        wbuf = A.alloc("wbuf", 8 * 1024, BF16)
        wbuf3 = wbuf.v3(8)
        mstage = A.alloc("mstage", 8 * 256, F32)
        msq = A.alloc("msq", 8 * 256, BF16)
        memn = A.alloc("memn", 8 * 256, BF16)
        mln = A.alloc("mln", 256, F32)
        mrs = A.alloc("mrs", 256, F32)
        kmT = A.alloc("kmT", 8 * 256, BF16)
        vm = A.alloc("vm", 2 * 1024, BF16)
        kmT3, vm3, memn3 = kmT.v3(8), vm.v3(2), memn.v3(8)

        for b_ in vown:
            MSET(b_.v3(2)[:, :, 64:128], 1.0, [b_])
        for b_ in vctx:
            TS(b_.v3(2)[:, :, 64:128], ones.ap.rearrange("p (a b) -> p a b", a=2), vcol(C_FLAG), ALU.mult, [ones, vecs], [b_])

        DMA("pool", wbuf3, wview(w_xk_d), (), [wbuf], wbuf)
        ms3 = mstage.v3(8)
        DMA("sp", ms3, memT.rearrange("(c p) t -> p c t", p=128), (), [mstage], mstage)

        def mem_norm():
            norm_stats(ms3, 8, 256, 1024.0, [mstage], msq, mln, mrs)
            for c in range(8):
                STT(memn3[:, c, :], ms3[:, c, :], vcol(C_GMEM + c), mrs.ap[:, 0:256], ALU.mult, ALU.mult,
                    [mstage, vecs, mrs], [], pw=[memn])

        def mem_k():
            for oc in range(8):
                pb, ph, _ = nextps()
                for kc in range(8):
                    MM(ph[:, 0:256], wbuf3[:, kc, oc * 128:(oc + 1) * 128], memn3[:, kc, :], kc == 0, kc == 7, [wbuf, memn], [pb])
                EVAC(kmT3[:, oc, :], ph[:, 0:256], [pb], [], pw=[kmT])
            DMA("pool", wbuf3, wview(w_xv_d), (), [wbuf], wbuf)

        def mem_v():
            for mt in range(2):
                for hf_ in range(2):
                    pb, ph, _ = nextps()
                    for kc in range(8):
                        MM(ph[:, 0:512], memn3[:, kc, mt * 128:(mt + 1) * 128], wbuf3[:, kc, hf_ * 512:(hf_ + 1) * 512],
                           kc == 0, kc == 7, [wbuf, memn], [pb])
                    EVAC(vm3[:, mt, hf_ * 512:(hf_ + 1) * 512], ph[:, 0:512], [pb], [], pw=[vm])

        Trow = Ttp[0].ap.ap[0][0]

        def attention_pair(hp):
            MSET(acc3[:, :, 0:F0], 1.0, [acc])
            if hp + 1 < 4:
                load_T(hp + 1)
            Tt = Ttp[hp % 2]
            CP(qm3[0:64, 0, :], qT3[0:64, hp, :], [qT], [qm])
            CP(qm3[64:128, 1, :], qT3[64:128, hp, :], [qT], [qm])
            vcache = {}
            vrr = {"own": 0, "ctx": 0}

            def get_vtile(d, r, m0):
                key = (d, r, m0)
                if key in vcache:
                    return vcache[key]
                kind = "own" if m0 * d >= 2048 else "ctx"
                slots = vown if kind == "own" else vctx
                sl = slots[vrr[kind] % len(slots)]
                vrr[kind] += 1
                for k_ in [k_ for k_, v_ in vcache.items() if v_ is sl]:
                    del vcache[k_]
                pb, ph, phb = PS[6 + vtn[0] % 2]
                vtn[0] += 1
                TR(phb[:, 0:128], vT3[:, hp, cols(r + d * m0, d, 128)], ident.ap, [vT, ident], [pb])
                CP(sl.v3(2)[:, :, 0:64], phb[:, 0:128].rearrange("p (a b) -> p a b", a=2), [pb], [sl])
                vcache[key] = sl
                return sl

            itn = [0]
            vtn = [0]

            def item(d, di, r, n0, W, tiles, first):
                nt = len(tiles)
                it = itn[0]
                itn[0] += 1
                vts = [get_vtile(d, r, m0) for (m0, off) in tiles]
                q0 = r + d * n0 - E0
                pbs, phs, _ = PS[it % 3]
                for ti, (m0, off) in enumerate(tiles):
                    outap = phs[:, ti * 2 * W:(ti + 1) * 2 * W]
                    MM(outap, kT3[:, hp, cols(r + d * m0, d, 128)],
                       qm3[:, :, cols(q0, d, W)], True, True, [kT, qm], [pbs])
                tot = 2 * nt * W
                eb, pb_ = e_sb[it % 3], pT[it % 3]
                ACT(eb.ap[:, 0:tot], phs[:, 0:tot], AF.Exp, [pbs], [eb], scale=0.125)
                for hh in range(2):
                    toff = Tt.ap.offset + (hh * 3 + di) * 384 + tiles[0][1] + 128
                    hw_ = nt * W
                    if nt == 2:
                        tap = bass.AP(tensor=Tt.ap.tensor, offset=toff, ap=[[Trow, 128], [128, 2], [1, W]])
                        ev = eb.ap[:, 0:tot].rearrange("p (b a c) -> p b a c", b=2, a=2)[:, :, hh, :]
                        pv = pb_[hh].ap[:, 0:hw_].rearrange("p (b c) -> p b c", b=2)
                    else:
                        tap = bass.AP(tensor=Tt.ap.tensor, offset=toff, ap=[[Trow, 128], [1, W]])
                        ev = eb.ap[:, hh * W:(hh + 1) * W]
                        pv = pb_[hh].ap[:, 0:hw_]
                    TT(pv, ev, tap, ALU.mult, [eb, Tt], [pb_[hh]], eng="dve" if hh == 0 else "pool")
                return (it, d, q0, W, nt, vts, pb_, first)

            def item_b(ctx):
                it, d, q0, W, nt, vts, pb_, first = ctx
                pbo, pho, _ = PS[3 + it % 3]
                for hh in range(2):
                    for ti in range(nt):
                        MM(pho[:, hh * W:(hh + 1) * W], vts[ti].v3(2)[:, hh, :], pb_[hh].ap[:, ti * W:(ti + 1) * W],
                           ti == 0, ti == nt - 1, [vts[ti], pb_[hh]], [pbo])
                dst = acc3[:, :, cols(q0, d, W)]
                src = pho[:, 0:2 * W].rearrange("p (a c) -> p a c", a=2)
                if first:
                    CP(dst, src, [pbo], [acc])
                else:
                    TT(dst, src, dst, ALU.add, [pbo, acc], [acc])

            pend = []

            def run_item(*a):
                pend.append(item(*a))
                if len(pend) > 2:
                    item_b(pend.pop(0))

            for di, d in enumerate((1, 4, 16)):
                if d == 16 and hp + 1 < 4:
                    exp_T(hp + 1)
                n_own = 2048 // d
                sub = 4096 // d
                for r in range(d):
                    hq_ = [t for t in (2046, 2047) if t % d == r]
                    if hq_:
                        n0 = (hq_[0] - r) // d
                        W = len(hq_)
                        mB = (n0 // 128) * 128
                        tiles = [(mB, n0 - mB)]
                        if mB - 128 >= 0:
                            tiles.append((mB - 128, n0 - mB + 128))
                        run_item(d, di, r, n0, W, tiles, d == 1)
                    for n0 in range(n_own, sub, 128):
                        run_item(d, di, r, n0, 128, [(n0, 0), (n0 - 128, 128)], d == 1)
            while pend:
                item_b(pend.pop(0))
            for hh in range(2):
                ACT(acc3[64:128, hh, :], acc3[64:128, hh, :], AF.Ln, [acc, vecs], [acc], bias=vecs.ap[64:128, C_TINY:C_TINY + 1])
                for cb in range(NEB):
                    c0 = cb * EBW
                    rb = rdb[cb % 2]
                    ACT(rb.ap[0:64, :], acc3[64:128, hh, c0:c0 + EBW], AF.Exp, [acc], [rb], scale=-1.0)
                    TT(attnT3[hh * 64:(hh + 1) * 64, hp, c0:c0 + EBW], acc3[0:64, hh, c0:c0 + EBW], rb.ap[0:64, :], ALU.mult,
                       [acc, rb], [], pw=[attnT])

        attention_pair(0)
        mem_norm()
        mem_k()
        attention_pair(1)
        mem_v()
        attention_pair(2)
        attention_pair(3)

        CK(3)
        A.release(kT, vT, qT, Ttp[0], Ttp[1], qm, acc, *e_sb, *[b_ for pr in pT for b_ in pr], rdb[0], rdb[1], wbuf, mstage, msq, memn, mln, mrs,
                  *vown, *vctx)

        w_out = A.alloc("w_out", 8 * 1024, BF16)
        w_out3 = w_out.v3(8)
        DMA("pool", w_out3, wview(w_out_d), (), [w_out], w_out)
        xres = [A.alloc(f"xres{j}", 8 * EBW, F32) for j in range(NEB)]
        xres3 = [b_.v3(8) for b_ in xres]
        xc = [[Buf(f"xc{j}_{c}", None) for c in range(8)] for j in range(NEB)]
        for j in range(NEB):
            s = E0 + j * EBW
            DMA("sp", xres3[j], xT.rearrange("(c p) t -> p c t", p=128)[:, :, s:s + EBW], (), [xres[j]] + xc[j], xres[j])
        w_xq = A.alloc("w_xq", 8 * 1024, BF16)
        w_xo = A.alloc("w_xo", 8 * 1024, BF16)
        w_xq3, w_xo3 = w_xq.v3(8), w_xo.v3(8)
        DMA("pool", w_xq3, wview(w_xq_d), (), [w_xq], w_xq)
        DMA("pool", w_xo3, wview(w_xo_d), (), [w_xo], w_xo)
        mixed = [A.alloc(f"mixed{i}", 8 * EBW, BF16) for i in range(2)]
        sq = A.alloc("sq2", 8 * EBW, BF16)
        lnb = A.alloc("lnb2", EBW, F32)
        rsa = A.alloc("rsa", EBW, F32)
        rsc = A.alloc("rsc", EBW, F32)

        def nm_1c(j):
            c0 = j * EBW
            mx3 = mixed[j % 2].v3(8)
            norm_stats(attnT3[:, :, c0:c0 + EBW], 4, EBW, 512.0, [attnT], sq, lnb, rsa)
            for c in range(4):
                STT(mx3[:, c, :], attnT3[:, c, c0:c0 + EBW], vcol(C_GATT + c), rsa.ap, ALU.mult, ALU.mult,
                    [attnT, vecs, rsa], [], pw=[mixed[j % 2]])
            norm_stats(convT3[:, :, c0:c0 + EBW], 4, EBW, 512.0, [convT], sq, lnb, rsc)
            for c in range(4):
                STT(mx3[:, 4 + c, :], convT3[:, c, c0:c0 + EBW], vcol(C_GCONV + c), rsc.ap, ALU.mult, ALU.mult,
                    [convT, vecs, rsc], [], pw=[mixed[j % 2]])

        def op_1c(j):
            mx3 = mixed[j % 2].v3(8)
            for oc in range(8):
                pb, ph, _ = nextps()
                for kc in range(8):
                    MM(ph[:, 0:EBW], w_out3[:, kc, oc * 128:(oc + 1) * 128], mx3[:, kc, :], kc == 0, kc == 7,
                       [w_out, mixed[j % 2]], [pb])
                TT(xres3[j][:, oc, :], ph[:, 0:EBW], xres3[j][:, oc, :], ALU.add, [pb, xc[j][oc]], [xc[j][oc]])

        nm_1c(0)
        for j in range(NEB):
            if j + 1 < NEB:
                nm_1c(j + 1)
            op_1c(j)

        CK(4)
        A.release(attnT, convT, w_out, mixed[0], mixed[1], rsa, rsc)

        hqb = [A.alloc(f"hq{i}", 8 * EBW, BF16) for i in range(2)]
        qxb = [A.alloc(f"qx{i}", 8 * EBW, BF16) for i in range(2)]
        pTx = [A.alloc(f"pTx{i}", 2 * EBW, BF16) for i in range(2)]
        rsx = A.alloc("rsx", EBW, F32)
        rdx = A.alloc("rdx", EBW, F32)
        slots = [((A.alloc(f"wupg{i}", 8 * 128, BF16), A.alloc(f"wupv{i}", 8 * 128, BF16)), A.alloc(f"wdn{i}", 1024, BF16))
                 for i in range(NSLOT)]
        pair_slot = {}

        def load_pair(p):
            su, sd = slots[p % NSLOT]
            DMA("pool", su[0].v3(8), wview(w_up_d)[:, :, p * 128:(p + 1) * 128], (), [su[0]], su[0])
            DMA("pool", su[1].v3(8), wview(w_up_d)[:, :, 2816 + p * 128:2816 + (p + 1) * 128], (), [su[1]], su[1])
            DMA("pool", sd.ap, w_down_d[p * 128:(p + 1) * 128, :], (), [sd], sd)
            pair_slot[p] = (su, sd)

        for p in range(NSLOT):
            load_pair(p)
        N = EBW

        def n_p2(j):
            hq, hq3 = hqb[j % 2], hqb[j % 2].v3(8)
            norm_stats(xres3[j], 8, N, 1024.0, xc[j], sq, lnb, rsx)
            for c in range(8):
                STT(hq3[:, c, :], xres3[j][:, c, :], vcol(C_GX + c), rsx.ap, ALU.mult, ALU.mult, [xc[j][c], vecs, rsx], [], pw=[hq])

        def q_p2(j, ocs):
            hq, hq3 = hqb[j % 2], hqb[j % 2].v3(8)
            qx, qx3 = qxb[j % 2], qxb[j % 2].v3(8)
            for oc in ocs:
                pb, ph, _ = nextps()
                for kc in range(8):
                    MM(ph[:, 0:N], w_xq3[:, kc, oc * 128:(oc + 1) * 128], hq3[:, kc, :], kc == 0, kc == 7, [w_xq, hq], [pb])
                EVAC(qx3[:, oc, :], ph[:, 0:N], [pb], [], pw=[qx])

        def h_p2(j):
            ox, ox3 = hqb[j % 2], hqb[j % 2].v3(8)
            qx, qx3 = qxb[j % 2], qxb[j % 2].v3(8)

            def s_stage(hd):
                pt = pTx[hd % 2]
                pt3 = pt.v3(2)
                for mt in range(2):
                    pb, ph, _ = nextps()
                    for dc in range(2):
                        MM(ph[:, 0:N], kmT3[:, 2 * hd + dc, mt * 128:(mt + 1) * 128], qx3[:, 2 * hd + dc, :], dc == 0, dc == 1,
                           [kmT, qx], [pb])
                    ACT(pt3[:, mt, :], ph[:, 0:N], AF.Exp, [pb], [pt], scale=1.0 / 16.0)

            def r_stage(hd):
                pt = pTx[hd % 2]
                pt3 = pt.v3(2)
                pbd, phd, _ = nextps()
                for mt in range(2):
                    MM(phd[:, 0:N], ones.ap, pt3[:, mt, :], mt == 0, mt == 1, [ones, pt], [pbd])
                ACT(rdx.ap, phd[:, 0:N], AF.Ln, [pbd], [rdx])
                ACT(rdx.ap, rdx.ap, AF.Exp, [rdx], [rdx], scale=-1.0)
                for dc in range(2):
                    pb, ph, _ = nextps()
                    for mt in range(2):
                        MM(ph[:, 0:N], vm3[:, mt, (2 * hd + dc) * 128:(2 * hd + dc + 1) * 128], pt3[:, mt, :], mt == 0, mt == 1,
                           [vm, pt], [pb])
                    TT(ox3[:, 2 * hd + dc, :], ph[:, 0:N], rdx.ap, ALU.mult, [pb, rdx], [], pw=[ox])

            def fill(i):
                if j + 1 < NEB:
                    q_p2(j + 1, (2 * i, 2 * i + 1))

            s_stage(0)
            s_stage(1)
            r_stage(0)
            s_stage(2)
            fill(0)
            r_stage(1)
            s_stage(3)
            fill(1)
            r_stage(2)
            fill(2)
            r_stage(3)
            fill(3)

        def o_p2(j):
            ox, ox3 = hqb[j % 2], hqb[j % 2].v3(8)
            for oc in range(8):
                pb, ph, _ = nextps()
                for kc in range(8):
                    MM(ph[:, 0:N], w_xo3[:, kc, oc * 128:(oc + 1) * 128], ox3[:, kc, :], kc == 0, kc == 7, [w_xo, ox], [pb])
                TT(xres3[j][:, oc, :], ph[:, 0:N], xres3[j][:, oc, :], ALU.add, [pb, xc[j][oc]], [xc[j][oc]])

        n_p2(0)
        q_p2(0, range(8))
        for j in range(NEB):
            if j + 1 < NEB:
                n_p2(j + 1)
            h_p2(j)
            o_p2(j)

        CK(5)
        A.release(hqb[0], hqb[1], qxb[0], qxb[1], pTx[0], pTx[1], rsx, rdx, w_xq, w_xo, kmT, vm)

        hf = A.alloc("hf", 8 * (NFB * FBW), BF16)
        hf3 = hf.v3(8)
        rsf = A.alloc("rsf", EBW, F32)
        xfl = [b_.ap for b_ in xres]

        def xcols(c0, n):
            out = []
            c = c0
            while c < c0 + n:
                j = c // EBW
                l0 = c - j * EBW
                cnt = min(EBW - l0, c0 + n - c)
                out.append((j, l0, cnt, c - c0))
                c += cnt
            return out

        def hf_piece(j):
            lo = max(F0, j * EBW)
            n = (j + 1) * EBW - lo
            l0 = lo - j * EBW
            src = xres3[j][:, :, l0:l0 + n]
            norm_stats(src, 8, n, 1024.0, xc[j], sq, lnb, rsf)
            for c in range(8):
                STT(hf3[:, c, lo - F0:lo - F0 + n], xres3[j][:, c, l0:l0 + n], vcol(C_GFFN + c), rsf.ap[:, 0:n], ALU.mult, ALU.mult,
                    [xc[j][c], vecs, rsf], [], pw=[hfp[j]])

        hfp = [Buf(f"hfp{j}", None) for j in range(NEB)]
        hf_piece(0)

        TS(hf3[:, :, 0:2], hf3[:, :, 0:2], vcol(C_FLAG), ALU.mult, [hfp[0], vecs], [hfp[0]], s2=0.0, op1=ALU.add)
        ost = A.alloc("ost", 8 * EBW, F32)
        rsz = A.alloc("rsz", EBW, F32)
        tg = [A.alloc(f"tg{i}", EBW, F32) for i in range(2)]
        tv = [A.alloc(f"tv{i}", EBW, F32) for i in range(2)]
        sg = [A.alloc(f"sg{i}", EBW, F32) for i in range(2)]
        actb = [A.alloc(f"actb{i}", EBW, BF16) for i in range(10)]
        xtmp = [A.alloc(f"xtmp{i}", EBW, F32) for i in range(3)]

        def fblock(j):
            lo = max(32, j * EBW)
            return lo - F0, (j + 1) * EBW - lo

        def ffn_up(k, pairs, j):
            o, n = fblock(j)
            acts = []
            pend_mult = None
            for pi, p in enumerate(pairs):
                su, sd = pair_slot[p]
                hs = []
                for half, fc in ((0, p), (1, 22 + p)):
                    pb, ph, _ = nextps()
                    for kc in range(8):
                        MM(ph[:, 0:n + 2], su[half].v3(8)[:, kc, :], hf3[:, kc, o - 2:o + n], kc == 0, kc == 7,
                           [su[half], hf] + hfp[max(0, j - 1):j + 1], [pb])
                    t = (tg if half == 0 else tv)[pi % 2]
                    wc = C_WFC + fc * 3
                    ACT(t.ap[:, 0:n], ph[:, 0:n], AF.Identity, [pb, vecs], [t], scale=vcol(wc), bias=vcol(C_BFC + fc))
                    hs.append((pb, ph, t, wc))
                def stt(h, tap_):
                    pb, ph, t, wc = hs[h]
                    STT(t.ap[:, 0:n], ph[:, tap_:n + tap_], vcol(wc + tap_), t.ap[:, 0:n], ALU.mult, ALU.add, [pb, vecs, t], [t])
                stt(0, 1)
                stt(1, 1)
                stt(0, 2)
                s_ = sg[pi % 2]
                ACT(s_.ap[:, 0:n], hs[0][2].ap[:, 0:n], AF.Silu, [hs[0][2]], [s_])
                if pend_mult is not None:
                    pend_mult()
                stt(1, 2)
                ab = actb[(k % 2) * 5 + pi]

                def mult(s_=s_, tv_=hs[1][2], ab=ab):
                    TT(ab.ap[:, 0:n], s_.ap[:, 0:n], tv_.ap[:, 0:n], ALU.mult, [s_, tv_], [ab])
                pend_mult = mult
                acts.append(ab)
            pend_mult()
            return acts

        xflip = [0]

        def ffn_down(pairs, j, acts):
            o, n = fblock(j)
            for oc in range(8):
                pb, ph, _ = nextps()
                for pi, p in enumerate(pairs):
                    su, sd = pair_slot[p]
                    MM(ph[:, 0:n], sd.ap[:, oc * 128:(oc + 1) * 128], acts[pi].ap[:, 0:n], pi == 0, pi == len(pairs) - 1,
                       [sd, acts[pi]], [pb])
                if oc % 2 == 0:
                    for (jb, l0, cnt, do) in xcols(F0 + o, n):
                        TT(xres3[jb][:, oc, l0:l0 + cnt], ph[:, do:do + cnt], xres3[jb][:, oc, l0:l0 + cnt], ALU.add,
                           [pb, xc[jb][oc]], [xc[jb][oc]])
                else:
                    xt = xtmp[xflip[0] % 3]
                    xflip[0] += 1
                    ACT(xt.ap[:, 0:n], ph[:, 0:n], AF.Copy, [pb], [xt])
                    for (jb, l0, cnt, do) in xcols(F0 + o, n):
                        TT(xres3[jb][:, oc, l0:l0 + cnt], xt.ap[:, do:do + cnt], xres3[jb][:, oc, l0:l0 + cnt], ALU.add,
                           [xt, xc[jb][oc]], [xc[jb][oc]], eng="pool")

        def final_block(j):
            lo = max(32, j * EBW)
            n = (j + 1) * EBW - lo
            l0 = lo - j * EBW
            norm_stats(xres3[j][:, :, l0:l0 + n], 8, n, 1024.0, xc[j], sq, lnb, rsz)
            o3 = ost.ap[:, 0:8 * n].rearrange("p (a b) -> p a b", a=8)
            for c in range(8):
                STT(o3[:, c, :], xres3[j][:, c, l0:l0 + n], vcol(C_GFIN + c), rsz.ap[:, 0:n], ALU.mult, ALU.mult,
                    [xc[j][c], vecs, rsz], [], pw=[ost])
            DMA("sp", yT.rearrange("(c p) t -> p c t", p=128)[:, :, lo - 32:lo - 32 + n], o3, [ost], [], outsem[j % 2])

        steps = []
        p0 = 0
        for gi, gsz in enumerate(GROUPS):
            for j in range(NFB):
                steps.append((list(range(p0, p0 + gsz)), j, j == NFB - 1, gi == len(GROUPS) - 1))
            p0 += gsz
        nl = [NSLOT]
        done_pairs = [0]

        def after_down(pairs, last):
            if not last:
                return
            done_pairs[0] = pairs[-1] + 1
            while nl[0] < NPAIR and nl[0] - done_pairs[0] < NSLOT:
                load_pair(nl[0])
                nl[0] += 1

        prev = None
        for k, (pairs, j, last, lastg) in enumerate(steps):
            if k + 1 < NEB:
                hf_piece(k + 1)
            acts = ffn_up(k, pairs, j)
            if prev is not None:
                ffn_down(prev[0], prev[1], prev[4])
                after_down(prev[0], prev[2])
                if prev[3] and prev[1] >= 1:
                    final_block(prev[1] - 1)
            prev = (pairs, j, last, lastg, acts)
        ffn_down(prev[0], prev[1], prev[4])
        final_block(NEB - 2)
        final_block(NEB - 1)

        CK(6)

    try:
        record()
    except _Stop:
        pass

    S.finalize()
    sems = {}
    for e in S.ENGS:
        for ep in range(S.nepoch[e]):
            sems[(e, ep)] = stack.enter_context(nc.semaphore(f"s_{e}{ep}"))
    for i, b in enumerate(S.dma_bufs):
        sems[("d", id(b))] = stack.enter_context(nc.semaphore(f"d{i}"))
    block = stack.enter_context(nc.Block())

    @block.tensor
    def _(t):
        S.emit("pe", t, sems)

    @block.scalar
    def _(a):
        S.emit("act", a, sems)

    @block.vector
    def _(v):
        S.emit("dve", v, sems)

    @block.gpsimd
    def _(g):
        S.emit("pool", g, sems)

    @block.sync
    def _(sy):
        S.emit("sp", sy, sems, final_waits=outsem)

    stack.close()
    return nc, A.peak, {e: len(S.ops[e]) for e in S.ENGS}


def _t5_bucket_np(dist):
    dist = np.asarray(dist)
    d = np.maximum(dist, 1).astype(np.float32)
    large = 16 + (np.log(d / np.float32(16)) / np.float32(math.log(2048 / 16)) * np.float32(16)).astype(np.int32)
    large = np.minimum(large, 31)
    return np.where(dist < 16, dist, large)


def _bias_tables(rel_bias):
    p = np.arange(128)[:, None]
    c = np.arange(384)[None, :]
    delta = c - p - 128
    valid = (delta >= 0) & (delta <= 128)
    out = np.empty((128, 8, 3, 384), np.float32)
    for di, d in enumerate((1, 4, 16)):
        bucket = _t5_bucket_np(np.clip(delta, 0, 128) * d)
        for h in range(8):
            out[:, h, di, :] = np.where(valid, rel_bias[h][bucket], np.float32(-30000.0))
    return np.ascontiguousarray(out.reshape(128, 24 * 384))


def _chunkcols(v):
    return np.asarray(v, np.float32).reshape(-1, 128).T


_CACHE = {}


def make_in_maps(x, mem, rel_bias, g_mix, w_in, w_short_conv, g_attn_out, g_conv_out, w_out,
                 g_xattn, g_mem, w_xq, w_xk, w_xv, w_xo, g_ffn, w_up, w_ffn_conv, b_ffn_conv,
                 w_down, g_final, cores=range(8)):
    f = lambda a: np.ascontiguousarray(np.asarray(a, np.float32))
    x, mem = f(x), f(mem)
    rel_bias = f(rel_bias)
    vecs = np.zeros((128, NV), np.float32)
    vecs[:, C_GMIX:C_GMIX + 8] = _chunkcols(g_mix[0])
    vecs[:, C_GX:C_GX + 8] = _chunkcols(g_xattn[0])
    vecs[:, C_GMEM:C_GMEM + 8] = _chunkcols(g_mem[0])
    vecs[:, C_GFFN:C_GFFN + 8] = _chunkcols(g_ffn[0])
    vecs[:, C_GFIN:C_GFIN + 8] = _chunkcols(g_final)
    vecs[:, C_GATT:C_GATT + 4] = _chunkcols(g_attn_out[0])
    vecs[:, C_GCONV:C_GCONV + 4] = _chunkcols(g_conv_out[0])
    wsc = f(w_short_conv)[0]
    vecs[:, C_WSC:C_WSC + 12] = wsc.reshape(3, 4, 128).transpose(2, 1, 0).reshape(128, 12)
    wfc = f(w_ffn_conv)[0]
    vecs[:, C_WFC:C_WFC + 132] = wfc.reshape(3, 44, 128).transpose(2, 1, 0).reshape(128, 132)
    vecs[:, C_BFC:C_BFC + 44] = _chunkcols(f(b_ffn_conv)[0])
    vecs[:, C_EPS] = EPS
    vecs[:, C_TINY] = 1e-18
    tbh = _bias_tables(rel_bias)
    ident = np.eye(128, dtype=np.float32)
    shared = dict(tbh=tbh, ident=ident, w_in=f(w_in)[0], w_out=f(w_out)[0], w_xq=f(w_xq)[0], w_xk=f(w_xk)[0],
                  w_xv=f(w_xv)[0], w_xo=f(w_xo)[0], w_up=f(w_up)[0], w_down=f(w_down)[0])
    in_maps = []
    for core in cores:
        b, half = core // 2, core % 2
        xTl = np.zeros((1024, 4096), np.float32)
        if half == 1:
            xTl[:, :] = x[b].T
        else:
            xTl[:, 2048:] = x[b, 0:2048].T
        v = vecs.copy()
        v[:, C_FLAG] = float(half)
        m = dict(shared)
        m.update(xT=xTl, memT=np.ascontiguousarray(mem[b].T), vecs=v)
        in_maps.append(m)
    return in_maps


def kernel(**inputs):
    if "nc" not in _CACHE:
        _CACHE["nc"] = build_program()[0]
    nc = _CACHE["nc"]
    in_maps = make_in_maps(**inputs)
    res = run_bass_kernel_spmd(nc, in_maps, core_ids=list(range(8)))
    out = np.empty((4, 4096, 1024), np.float32)
    for core in range(8):
        b, half = core // 2, core % 2
        out[b, half * 2048:(half + 1) * 2048, :] = res.results[core]["yT"].T
    return out
```

```python
import math
from contextlib import ExitStack
import numpy as np
import concourse.bass as bass
import concourse.mybir as mybir
from concourse.bass_utils import run_bass_kernel_spmd

F32, BF16 = mybir.dt.float32, mybir.dt.bfloat16
AF = mybir.ActivationFunctionType
ALU = mybir.AluOpType

P = 128
E0, NE, EBW, NEB = 2016, 2080, 416, 5
CBW, NCB = 504, 4
F0, FBW, NFB = 30, 410, 5
NPAIR = 22
GROUPS = [5, 4, 5, 4, 4]
NSLOT = 9
EPS = 1e-6
NV = 240
C_GMIX, C_GX, C_GMEM, C_GFFN, C_GFIN, C_GATT, C_GCONV, C_WSC, C_WFC, C_BFC, C_FLAG, C_EPS, C_TINY = \
    0, 8, 16, 24, 32, 40, 44, 48, 60, 192, 236, 237, 238
ARENA_BYTES = 211968
SAME_ENG_SYNC = True
SEM_EPOCH = 30000


class Buf:
    __slots__ = ("name", "ap", "lastw", "readers", "sem", "dcount", "start", "end", "pw")

    def __init__(self, name, ap, start=0, end=0):
        self.name, self.ap, self.start, self.end = name, ap, start, end
        self.lastw, self.readers, self.sem, self.dcount = None, [], None, 0
        self.pw = []

    def v3(self, a):
        return self.ap.rearrange("p (a b) -> p a b", a=a)


class Op:
    __slots__ = ("eng", "fn", "deps", "sig", "val", "dma_buf", "dma_val")


class Sched:
    ENGS = ("pe", "act", "dve", "pool", "sp")

    def __init__(self):
        self.ops = {e: [] for e in self.ENGS}
        self.dma_bufs = []

    def op(self, eng, fn, reads=(), writes=(), dma=None, pwrites=()):
        o = Op()
        o.eng, o.fn, o.sig, o.val, o.dma_buf, o.dma_val = eng, fn, False, 0, None, 0
        deps = set()
        for b in reads:
            if b.lastw is not None:
                deps.add(b.lastw)
            deps.update(b.pw)
        for b in writes:
            if b.lastw is not None:
                deps.add(b.lastw)
            deps.update(b.pw)
            deps.update(b.readers)
        for b in pwrites:
            if b.lastw is not None:
                deps.add(b.lastw)
            deps.update(b.readers)
        keep = []
        for d in deps:
            if d.dma_buf is None and d.eng == eng and (eng == "pe" or not SAME_ENG_SYNC):
                continue
            keep.append(d)
            d.sig = True
        o.deps = keep
        for b in reads:
            b.readers.append(o)
        for b in writes:
            b.lastw = o
            b.readers = []
            b.pw = []
        for b in pwrites:
            b.pw.append(o)
        if dma is not None:
            if dma.dcount == 0:
                self.dma_bufs.append(dma)
            dma.dcount += 1
            o.dma_buf, o.dma_val = dma, 16 * dma.dcount
        self.ops[eng].append(o)
        return o

    def finalize(self):
        self.nepoch = {}
        for e in self.ENGS:
            n = 0
            for o in self.ops[e]:
                if o.dma_buf is None and o.sig:
                    n += 1
                    o.val = n
            self.nepoch[e] = max(1, (n + SEM_EPOCH - 1) // SEM_EPOCH)

    def sigpair(self, o):
        if o.dma_buf is not None:
            return ("d", id(o.dma_buf)), o.dma_val
        ep = (o.val - 1) // SEM_EPOCH
        return (o.eng, ep), o.val - ep * SEM_EPOCH

    def emit(self, eng, engobj, sems, final_waits=()):
        waited = {}
        for o in self.ops[eng]:
            need = {}
            for d in o.deps:
                k, v = self.sigpair(d)
                if v > need.get(k, 0):
                    need[k] = v
            for k, v in need.items():
                if waited.get(k, 0) < v:
                    engobj.wait_ge(sems[k], v)
                    waited[k] = v
            if o.dma_buf is not None and o.dma_val > 16:
                k = ("d", id(o.dma_buf))
                if waited.get(k, 0) < o.dma_val - 16:
                    engobj.wait_ge(sems[k], o.dma_val - 16)
                    waited[k] = o.dma_val - 16
            inst = o.fn(engobj)
            if o.dma_buf is not None:
                inst.then_inc(sems[("d", id(o.dma_buf))], 16)
            elif o.sig:
                k, _ = self.sigpair(o)
                inst.then_inc(sems[k], 1)
        for b in final_waits:
            if b.dcount == 0:
                continue
            engobj.wait_ge(sems[("d", id(b))], 16 * b.dcount)


class Arena:
    def __init__(self, handle, nbytes):
        self.h32 = handle
        self.hbf = handle.bitcast(BF16)
        self.free = [(0, nbytes)]
        self.dead = []
        self.peak = 0
        self.used = 0

    def alloc(self, name, cols, dt):
        esz = 4 if dt == F32 else 2
        nb = (cols * esz + 63) // 64 * 64
        for i, (s, e) in enumerate(self.free):
            if e - s >= nb:
                self.free[i] = (s + nb, e)
                if self.free[i][0] == self.free[i][1]:
                    del self.free[i]
                break
        else:
            raise RuntimeError(f"arena OOM for {name} ({nb} B); free={self.free}")
        if dt == F32:
            ap = self.h32[:, s // 4: s // 4 + cols]
        else:
            ap = self.hbf[:, s // 2: s // 2 + cols]
        b = Buf(name, ap, s, s + nb)
        inh = []
        for (ds, de, db) in self.dead:
            if ds < b.end and b.start < de:
                if db.lastw is not None:
                    inh.append(db.lastw)
                inh.extend(db.pw)
                inh.extend(db.readers)
        b.readers = inh
        self.used += nb
        self.peak = max(self.peak, self.used)
        return b

    def release(self, *bufs):
        for b in bufs:
            self.used -= b.end - b.start
            self.dead.append((b.start, b.end, b))
            self.free.append((b.start, b.end))
        self.free.sort()
        m = []
        for s, e in self.free:
            if m and m[-1][1] == s:
                m[-1] = (m[-1][0], e)
            else:
                m.append((s, e))
        self.free = m


def cols(start, step, count):
    return slice(start, start + step * (count - 1) + 1, step)


class _Stop(Exception):
    pass


def build_program(stop_after=99):
    nc = bass.Bass("TRN2", target_bir_lowering=False)

    def din(name, shape):
        return nc.dram_tensor(name, shape, F32, kind="ExternalInput").ap()

    xT = din("xT", [1024, 4096])
    memT = din("memT", [1024, 256])
    tbh = din("tbh", [128, 24 * 384])
    vecs_d = din("vecs", [128, NV])
    ident_d = din("ident", [128, 128])
    w_in_d = din("w_in", [1024, 3072])
    w_out_d = din("w_out", [1024, 1024])
    w_xq_d = din("w_xq", [1024, 1024])
    w_xk_d = din("w_xk", [1024, 1024])
    w_xv_d = din("w_xv", [1024, 1024])
    w_xo_d = din("w_xo", [1024, 1024])
    w_up_d = din("w_up", [1024, 5632])
    w_down_d = din("w_down", [2816, 1024])
    yT = nc.dram_tensor("yT", [1024, 2048], F32, kind="ExternalOutput").ap()

    def wview(w):
        return w.rearrange("(kc p) n -> p kc n", p=128)

    S = Sched()
    stack = ExitStack()
    arena_h = stack.enter_context(nc.sbuf_tensor("arena", [128, ARENA_BYTES // 4], F32))
    A = Arena(arena_h, ARENA_BYTES)
    psall = stack.enter_context(nc.psum_tensor("psall", [128, 4096], F32))
    psall_bf = psall.bitcast(BF16)

    class PSB:
        def __init__(self, h, base):
            self.h, self.base = h, base

        def __getitem__(self, idx):
            p, c = idx
            return self.h[p, slice(self.base + (c.start or 0), self.base + c.stop, c.step)]

    PS = [(Buf(f"ps{i}", None), PSB(psall, i * 512), PSB(psall_bf, i * 1024)) for i in range(8)]
    psi = [0]

    def nextps():
        r = PS[psi[0] % 8]
        psi[0] += 1
        return r

    outsem = [Buf("outsem0", None), Buf("outsem1", None)]

    def MM(out, lhsT, rhs, start, stop, rd, wr):
        return S.op("pe", lambda e: e.matmul(out, lhsT=lhsT, rhs=rhs, start=start, stop=stop), rd, wr)

    def TR(out, in_, ident, rd, wr):
        return S.op("pe", lambda e: e.transpose(out, in_, ident), rd, wr)

    def ACT(out, in_, func, rd, wr, scale=None, bias=None, pw=()):
        kw = {}
        if scale is not None:
            kw["scale"] = scale
        if bias is not None:
            kw["bias"] = bias
        return S.op("act", lambda e: e.activation(out=out, in_=in_, func=func, **kw), rd, wr, pwrites=pw)

    def TT(out, in0, in1, op, rd, wr, eng="dve", pw=()):
        return S.op(eng, lambda e: e.tensor_tensor(out=out, in0=in0, in1=in1, op=op), rd, wr, pwrites=pw)

    def TS(out, in0, s1, op0, rd, wr, s2=None, op1=None, eng="dve"):
        if op1 is None:
            return S.op(eng, lambda e: e.tensor_scalar(out=out, in0=in0, scalar1=s1, scalar2=None, op0=op0), rd, wr)
        return S.op(eng, lambda e: e.tensor_scalar(out=out, in0=in0, scalar1=s1, scalar2=s2, op0=op0, op1=op1), rd, wr)

    def STT(out, in0, sc, in1, op0, op1, rd, wr, pw=()):
        return S.op("dve", lambda e: e.scalar_tensor_tensor(out=out, in0=in0, scalar=sc, in1=in1, op0=op0, op1=op1), rd, wr,
                    pwrites=pw)

    def CP(out, in_, rd, wr, eng="dve", pw=()):
        if eng == "act":
            return ACT(out, in_, AF.Copy, rd, wr, pw=pw)
        return S.op(eng, lambda e: e.tensor_copy(out=out, in_=in_), rd, wr, pwrites=pw)

    def MSET(ap, val, wr, eng="dve"):
        return S.op(eng, lambda e: e.memset(ap, val), (), wr)

    def DMA(eng, out, in_, rd, wr, dmabuf):
        return S.op(eng, lambda e: e.dma_start(out=out, in_=in_), rd, wr, dma=dmabuf)

    evac_flip = [0]

    def EVAC(out, in_, rd, wr, pw=()):
        evac_flip[0] ^= 1
        return CP(out, in_, rd, wr, eng="act" if evac_flip[0] else "dve", pw=pw)

    def CK(n):
        if n >= stop_after:
            raise _Stop()

    def record():
        vecs = A.alloc("vecs", NV, F32)
        ones = A.alloc("ones", 128, BF16)
        ident = A.alloc("ident", 128, BF16)
        DMA("sp", vecs.ap, vecs_d[:, :], (), [vecs], vecs)
        DMA("pool", ident.ap, ident_d[:, :], (), [ident], ident)
        MSET(ones.ap, 1.0, [ones])

        def vcol(c):
            return vecs.ap[:, c:c + 1]

        def norm_stats(src_ap3, nch, N, dim, srcbufs, sq, lnb, rstd):
            sqv = sq.ap[:, 0:nch * N].rearrange("p (a b) -> p a b", a=nch)
            ACT(sqv, src_ap3, AF.Square, srcbufs, [sq])
            pb, ph, _ = nextps()
            for c in range(nch):
                MM(ph[:, 0:N], ones.ap, sqv[:, c, :], c == 0, c == nch - 1, [ones, sq], [pb])
            ACT(lnb.ap[:, 0:N], ph[:, 0:N], AF.Ln, [pb, vecs], [lnb], scale=1.0 / dim, bias=vcol(C_EPS))
            ACT(rstd.ap[:, 0:N], lnb.ap[:, 0:N], AF.Exp, [lnb], [rstd], scale=-0.5)

        w_in = A.alloc("w_in", 8 * 3072, BF16)
        w_in3 = w_in.v3(8)
        w_in_g = {}

        def load_w_in_group(g, after=()):
            gb = Buf(f"w_in_g{g}", None)
            DMA("pool", w_in3[:, :, g * 512:(g + 1) * 512], wview(w_in_d)[:, :, g * 512:(g + 1) * 512], after, [gb], gb)
            w_in_g[g] = gb

        prev_g = None
        for g_ in (1, 2, 0, 3, 4, 5):
            load_w_in_group(g_, [w_in_g[prev_g]] if prev_g is not None else ())
            prev_g = g_
        kT = A.alloc("kT", 4 * 4096, BF16)
        vT = A.alloc("vT", 4 * 4096, BF16)
        qT = A.alloc("qT", 4 * NE, BF16)
        convT = A.alloc("convT", 4 * NE, BF16)
        kT3, vT3, qT3, convT3 = kT.v3(4), vT.v3(4), qT.v3(4), convT.v3(4)
        xs = A.alloc("xs", 8 * CBW, F32)
        sq = A.alloc("sq", 8 * CBW, BF16)
        hN = [A.alloc(f"hN{i}", 8 * CBW, BF16) for i in range(2)]
        lnb = A.alloc("lnb", CBW, F32)
        rstd = [A.alloc(f"rstd{i}", CBW, F32) for i in range(2)]
        c_sb = [A.alloc(f"c_sb{i}", EBW, F32) for i in range(2)]
        ub = [A.alloc(f"ub{i}", EBW + 2, F32) for i in range(4)]
        t1 = [A.alloc(f"t1_{i}", EBW, F32) for i in range(2)]

        cblk = [(j * CBW, CBW, False) for j in range(NCB)]
        eblk = [(E0 + j * EBW, EBW, True) for j in range(NEB)]
        blocks = []
        for j in range(NEB):
            if j < NCB:
                blocks.append(cblk[j])
            blocks.append(eblk[j])
        first_e = blocks.index(eblk[0])

        def load_norm(j):
            s, N, isE = blocks[j]
            xs3 = xs.ap[:, 0:8 * N].rearrange("p (a b) -> p a b", a=8)
            DMA("sp", xs3, xT.rearrange("(c p) t -> p c t", p=128)[:, :, s:s + N], (), [xs], xs)
            norm_stats(xs3, 8, N, 1024.0, [xs], sq, lnb, rstd[j % 2])
            h3 = hN[j % 2].ap[:, 0:8 * N].rearrange("p (a b) -> p a b", a=8)
            for c in range(8):
                STT(h3[:, c, :], xs3[:, c, :], vcol(C_GMIX + c), rstd[j % 2].ap[:, 0:N], ALU.mult, ALU.mult,
                    [xs, vecs, rstd[j % 2]], [], pw=[hN[j % 2]])

        def proj_group(j, g, oc):
            s, N, isE = blocks[j]
            h3 = hN[j % 2].ap[:, 0:8 * N].rearrange("p (a b) -> p a b", a=8)
            pb, ph, _ = nextps()
            c0 = g * 512 + oc * 128
            for kc in range(8):
                MM(ph[:, 0:N], w_in3[:, kc, c0:c0 + 128], h3[:, kc, :], kc == 0, kc == 7,
                   [w_in_g[g], w_in, hN[j % 2]], [pb])
            return pb, ph

        for i in range(4):
            MSET(ub[i].ap[:, 0:2], 0.0, [ub[i]])

        def proj(j):
            s, N, isE = blocks[j]
            for g, dst, dst3 in ((1, kT, kT3), (2, vT, vT3)):
                for oc in range(4):
                    pb, ph = proj_group(j, g, oc)
                    EVAC(dst3[:, oc, s:s + N], ph[:, 0:N], [pb], [], pw=[dst])
            if not isE:
                return
            e0 = s - E0
            for oc in range(4):
                pb, ph = proj_group(j, 0, oc)
                EVAC(qT3[:, oc, e0:e0 + N], ph[:, 0:N], [pb], [], pw=[qT])
            for oc in range(4):
                pbb, phb = proj_group(j, 3, oc)
                pbc, phc = proj_group(j, 4, oc)
                pbx, phx = proj_group(j, 5, oc)
                cs = c_sb[oc % 2]
                tt = t1[oc % 2]
                u = ub[oc]
                ACT(cs.ap[:, 0:N], phc[:, 0:N], AF.Copy, [pbc], [cs])
                if j > first_e:
                    CP(u.ap[:, 0:2], u.ap[:, N:N + 2], [u], [u])
                TT(u.ap[:, 2:N + 2], phx[:, 0:N], cs.ap[:, 0:N], ALU.mult, [pbx, cs], [u])
                wc = C_WSC + oc * 3
                TS(tt.ap[:, 0:N], u.ap[:, 0:N], vcol(wc), ALU.mult, [u, vecs], [tt])
                STT(tt.ap[:, 0:N], u.ap[:, 1:N + 1], vcol(wc + 1), tt.ap[:, 0:N], ALU.mult, ALU.add, [u, vecs, tt], [tt])
                STT(tt.ap[:, 0:N], u.ap[:, 2:N + 2], vcol(wc + 2), tt.ap[:, 0:N], ALU.mult, ALU.add, [u, vecs, tt], [tt])
                TT(convT3[:, oc, e0:e0 + N], tt.ap[:, 0:N], phb[:, 0:N], ALU.mult, [tt, pbb], [], pw=[convT])

        CK(1)
        load_norm(0)
        proj(0)
        load_norm(1)
        for j in range(1, len(blocks)):
            if j + 1 < len(blocks):
                load_norm(j + 1)
            proj(j)

        CK(2)
        A.release(w_in, xs, sq, hN[0], hN[1], lnb, rstd[0], rstd[1], c_sb[0], c_sb[1], ub[0], ub[1], ub[2], ub[3], t1[0], t1[1])

        Ttp = [A.alloc(f"Ttp{i}", 6 * 384, BF16) for i in range(2)]

        def load_T(hp):
            tb_ = Ttp[hp % 2]
            DMA("pool", tb_.ap, tbh[:, hp * 6 * 384:(hp + 1) * 6 * 384], (), [tb_], tb_)

        def exp_T(hp):
            tb_ = Ttp[hp % 2]
            ACT(tb_.ap, tb_.ap, AF.Exp, [tb_], [tb_])

        load_T(0)
        exp_T(0)
        qm = A.alloc("qm", 2 * NE, BF16)
        qm3 = qm.v3(2)
        MSET(qm3[64:128, 0, :], 0.0, [qm])
        MSET(qm3[0:64, 1, :], 0.0, [qm])
        attnT = A.alloc("attnT", 4 * NE, BF16)
        attnT3 = attnT.v3(4)
        acc = A.alloc("acc", 2 * NE, F32)
        acc3 = acc.v3(2)
        e_sb = [A.alloc(f"e_sb{i}", 512, BF16) for i in range(3)]
        pT = [(A.alloc(f"pTa{i}", 256, BF16), A.alloc(f"pTb{i}", 256, BF16)) for i in range(3)]
        NVS = 8
        vown = [A.alloc(f"vown{i}", 256, BF16) for i in range(NVS)]
        NVC = 4
        vctx = [A.alloc(f"vctx{i}", 256, BF16) for i in range(NVC)]
        rdb = [A.alloc(f"rd{i}", EBW, F32) for i in range(2)]
        wbuf = A.alloc("wbuf", 8 * 1024, BF16)
        wbuf3 = wbuf.v3(8)
        mstage = A.alloc("mstage", 8 * 256, F32)
        msq = A.alloc("msq", 8 * 256, BF16)
        memn = A.alloc("memn", 8 * 256, BF16)
        mln = A.alloc("mln", 256, F32)
        mrs = A.alloc("mrs", 256, F32)
        kmT = A.alloc("kmT", 8 * 256, BF16)
        vm = A.alloc("vm", 2 * 1024, BF16)
        kmT3, vm3, memn3 = kmT.v3(8), vm.v3(2), memn.v3(8)

        for b_ in vown:
            MSET(b_.v3(2)[:, :, 64:128], 1.0, [b_])
        for b_ in vctx:
            TS(b_.v3(2)[:, :, 64:128], ones.ap.rearrange("p (a b) -> p a b", a=2), vcol(C_FLAG), ALU.mult, [ones, vecs], [b_])

        DMA("pool", wbuf3, wview(w_xk_d), (), [wbuf], wbuf)
        ms3 = mstage.v3(8)
        DMA("sp", ms3, memT.rearrange("(c p) t -> p c t", p=128), (), [mstage], mstage)

        def mem_norm():
            norm_stats(ms3, 8, 256, 1024.0, [mstage], msq, mln, mrs)
            for c in range(8):
                STT(memn3[:, c, :], ms3[:, c, :], vcol(C_GMEM + c), mrs.ap[:, 0:256], ALU.mult, ALU.mult,
                    [mstage, vecs, mrs], [], pw=[memn])

        def mem_k():
            for oc in range(8):
                pb, ph, _ = nextps()
                for kc in range(8):
                    MM(ph[:, 0:256], wbuf3[:, kc, oc * 128:(oc + 1) * 128], memn3[:, kc, :], kc == 0, kc == 7, [wbuf, memn], [pb])
                EVAC(kmT3[:, oc, :], ph[:, 0:256], [pb], [], pw=[kmT])
            DMA("pool", wbuf3, wview(w_xv_d), (), [wbuf], wbuf)

        def mem_v():
            for mt in range(2):
                for hf_ in range(2):
                    pb, ph, _ = nextps()
                    for kc in range(8):
                        MM(ph[:, 0:512], memn3[:, kc, mt * 128:(mt + 1) * 128], wbuf3[:, kc, hf_ * 512:(hf_ + 1) * 512],
                           kc == 0, kc == 7, [wbuf, memn], [pb])
                    EVAC(vm3[:, mt, hf_ * 512:(hf_ + 1) * 512], ph[:, 0:512], [pb], [], pw=[vm])

        Trow = Ttp[0].ap.ap[0][0]

        def attention_pair(hp):
            MSET(acc3[:, :, 0:F0], 1.0, [acc])
            if hp + 1 < 4:
                load_T(hp + 1)
            Tt = Ttp[hp % 2]
            CP(qm3[0:64, 0, :], qT3[0:64, hp, :], [qT], [qm])
            CP(qm3[64:128, 1, :], qT3[64:128, hp, :], [qT], [qm])
            vcache = {}
            vrr = {"own": 0, "ctx": 0}

            def get_vtile(d, r, m0):
                key = (d, r, m0)
                if key in vcache:
                    return vcache[key]
                kind = "own" if m0 * d >= 2048 else "ctx"
                slots = vown if kind == "own" else vctx
                sl = slots[vrr[kind] % len(slots)]
                vrr[kind] += 1
                for k_ in [k_ for k_, v_ in vcache.items() if v_ is sl]:
                    del vcache[k_]
                pb, ph, phb = PS[6 + vtn[0] % 2]
                vtn[0] += 1
                TR(phb[:, 0:128], vT3[:, hp, cols(r + d * m0, d, 128)], ident.ap, [vT, ident], [pb])
                EVAC(sl.v3(2)[:, :, 0:64], phb[:, 0:128].rearrange("p (a b) -> p a b", a=2), [pb], [sl])
                vcache[key] = sl
                return sl

            itn = [0]
            vtn = [0]

            def item(d, di, r, n0, W, tiles, first):
                nt = len(tiles)
                it = itn[0]
                itn[0] += 1
                vts = [get_vtile(d, r, m0) for (m0, off) in tiles]
                q0 = r + d * n0 - E0
                pbs, phs, _ = PS[it % 3]
                for ti, (m0, off) in enumerate(tiles):
                    outap = phs[:, ti * 2 * W:(ti + 1) * 2 * W]
                    MM(outap, kT3[:, hp, cols(r + d * m0, d, 128)],
                       qm3[:, :, cols(q0, d, W)], True, True, [kT, qm], [pbs])
                tot = 2 * nt * W
                eb, pb_ = e_sb[it % 3], pT[it % 3]
                ACT(eb.ap[:, 0:tot], phs[:, 0:tot], AF.Exp, [pbs], [eb], scale=0.125)
                for hh in range(2):
                    toff = Tt.ap.offset + (hh * 3 + di) * 384 + tiles[0][1] + 128
                    hw_ = nt * W
                    if nt == 2:
                        tap = bass.AP(tensor=Tt.ap.tensor, offset=toff, ap=[[Trow, 128], [128, 2], [1, W]])
                        ev = eb.ap[:, 0:tot].rearrange("p (b a c) -> p b a c", b=2, a=2)[:, :, hh, :]
                        pv = pb_[hh].ap[:, 0:hw_].rearrange("p (b c) -> p b c", b=2)
                    else:
                        tap = bass.AP(tensor=Tt.ap.tensor, offset=toff, ap=[[Trow, 128], [1, W]])
                        ev = eb.ap[:, hh * W:(hh + 1) * W]
                        pv = pb_[hh].ap[:, 0:hw_]
                    TT(pv, ev, tap, ALU.mult, [eb, Tt], [pb_[hh]], eng="dve" if hh == 0 else "pool")
                return (it, d, q0, W, nt, vts, pb_, first)

            def item_b(ctx):
                it, d, q0, W, nt, vts, pb_, first = ctx
                pbo, pho, _ = PS[3 + it % 3]
                for hh in range(2):
                    for ti in range(nt):
                        MM(pho[:, hh * W:(hh + 1) * W], vts[ti].v3(2)[:, hh, :], pb_[hh].ap[:, ti * W:(ti + 1) * W],
                           ti == 0, ti == nt - 1, [vts[ti], pb_[hh]], [pbo])
                dst = acc3[:, :, cols(q0, d, W)]
                src = pho[:, 0:2 * W].rearrange("p (a c) -> p a c", a=2)
                if first:
                    CP(dst, src, [pbo], [acc])
                else:
                    TT(dst, src, dst, ALU.add, [pbo, acc], [acc])

            pend = []

            def run_item(*a):
                pend.append(item(*a))
                if len(pend) > 2:
                    item_b(pend.pop(0))

            for di, d in enumerate((1, 4, 16)):
                if d == 16 and hp + 1 < 4:
                    exp_T(hp + 1)
                n_own = 2048 // d
                sub = 4096 // d
                for r in range(d):
                    hq_ = [t for t in (2046, 2047) if t % d == r]
                    if hq_:
                        n0 = (hq_[0] - r) // d
                        W = len(hq_)
                        mB = (n0 // 128) * 128
                        tiles = [(mB, n0 - mB)]
                        if mB - 128 >= 0:
                            tiles.append((mB - 128, n0 - mB + 128))
                        run_item(d, di, r, n0, W, tiles, d == 1)
                    for n0 in range(n_own, sub, 128):
                        run_item(d, di, r, n0, 128, [(n0, 0), (n0 - 128, 128)], d == 1)
            while pend:
                item_b(pend.pop(0))
            for hh in range(2):
                ACT(acc3[64:128, hh, :], acc3[64:128, hh, :], AF.Ln, [acc, vecs], [acc], bias=vecs.ap[64:128, C_TINY:C_TINY + 1])
                for cb in range(NEB):
                    c0 = cb * EBW
                    rb = rdb[cb % 2]
                    ACT(rb.ap[0:64, :], acc3[64:128, hh, c0:c0 + EBW], AF.Exp, [acc], [rb], scale=-1.0)
                    TT(attnT3[hh * 64:(hh + 1) * 64, hp, c0:c0 + EBW], acc3[0:64, hh, c0:c0 + EBW], rb.ap[0:64, :], ALU.mult,
                       [acc, rb], [], pw=[attnT])

        attention_pair(0)
        mem_norm()
        mem_k()
        attention_pair(1)
        mem_v()
        attention_pair(2)
        attention_pair(3)

        CK(3)
        A.release(kT, vT, qT, Ttp[0], Ttp[1], qm, acc, *e_sb, *[b_ for pr in pT for b_ in pr], rdb[0], rdb[1], wbuf, mstage, msq, memn, mln, mrs,
                  *vown, *vctx)

        w_out = A.alloc("w_out", 8 * 1024, BF16)
        w_out3 = w_out.v3(8)
        DMA("pool", w_out3, wview(w_out_d), (), [w_out], w_out)
        xres = [A.alloc(f"xres{j}", 8 * EBW, F32) for j in range(NEB)]
        xres3 = [b_.v3(8) for b_ in xres]
        xc = [[Buf(f"xc{j}_{c}", None) for c in range(8)] for j in range(NEB)]
        for j in range(NEB):
            s = E0 + j * EBW
            DMA("sp", xres3[j], xT.rearrange("(c p) t -> p c t", p=128)[:, :, s:s + EBW], (), [xres[j]] + xc[j], xres[j])
        w_xq = A.alloc("w_xq", 8 * 1024, BF16)
        w_xo = A.alloc("w_xo", 8 * 1024, BF16)
        w_xq3, w_xo3 = w_xq.v3(8), w_xo.v3(8)
        DMA("pool", w_xq3, wview(w_xq_d), (), [w_xq], w_xq)
        DMA("pool", w_xo3, wview(w_xo_d), (), [w_xo], w_xo)
        mixed = [A.alloc(f"mixed{i}", 8 * EBW, BF16) for i in range(2)]
        sq = A.alloc("sq2", 8 * EBW, BF16)
        lnb = A.alloc("lnb2", EBW, F32)
        rsa = A.alloc("rsa", EBW, F32)
        rsc = A.alloc("rsc", EBW, F32)

        def nm_1c(j):
            c0 = j * EBW
            mx3 = mixed[j % 2].v3(8)
            norm_stats(attnT3[:, :, c0:c0 + EBW], 4, EBW, 512.0, [attnT], sq, lnb, rsa)
            for c in range(4):
                STT(mx3[:, c, :], attnT3[:, c, c0:c0 + EBW], vcol(C_GATT + c), rsa.ap, ALU.mult, ALU.mult,
                    [attnT, vecs, rsa], [], pw=[mixed[j % 2]])
            norm_stats(convT3[:, :, c0:c0 + EBW], 4, EBW, 512.0, [convT], sq, lnb, rsc)
            for c in range(4):
                STT(mx3[:, 4 + c, :], convT3[:, c, c0:c0 + EBW], vcol(C_GCONV + c), rsc.ap, ALU.mult, ALU.mult,
                    [convT, vecs, rsc], [], pw=[mixed[j % 2]])

        def op_1c(j):
            mx3 = mixed[j % 2].v3(8)
            for oc in range(8):
                pb, ph, _ = nextps()
                for kc in range(8):
                    MM(ph[:, 0:EBW], w_out3[:, kc, oc * 128:(oc + 1) * 128], mx3[:, kc, :], kc == 0, kc == 7,
                       [w_out, mixed[j % 2]], [pb])
                TT(xres3[j][:, oc, :], ph[:, 0:EBW], xres3[j][:, oc, :], ALU.add, [pb, xc[j][oc]], [xc[j][oc]])

        nm_1c(0)
        for j in range(NEB):
            if j + 1 < NEB:
                nm_1c(j + 1)
            op_1c(j)

        CK(4)
        A.release(attnT, convT, w_out, mixed[0], mixed[1], rsa, rsc)

        hqb = [A.alloc(f"hq{i}", 8 * EBW, BF16) for i in range(2)]
        qxb = [A.alloc(f"qx{i}", 8 * EBW, BF16) for i in range(2)]
        pTx = [A.alloc(f"pTx{i}", 2 * EBW, BF16) for i in range(2)]
        rsx = A.alloc("rsx", EBW, F32)
        rdx = A.alloc("rdx", EBW, F32)
        slots = [((A.alloc(f"wupg{i}", 8 * 128, BF16), A.alloc(f"wupv{i}", 8 * 128, BF16)), A.alloc(f"wdn{i}", 1024, BF16))
                 for i in range(NSLOT)]
        pair_slot = {}

        def load_pair(p):
            su, sd = slots[p % NSLOT]
            DMA("pool", su[0].v3(8), wview(w_up_d)[:, :, p * 128:(p + 1) * 128], (), [su[0]], su[0])
            DMA("pool", su[1].v3(8), wview(w_up_d)[:, :, 2816 + p * 128:2816 + (p + 1) * 128], (), [su[1]], su[1])
            DMA("pool", sd.ap, w_down_d[p * 128:(p + 1) * 128, :], (), [sd], sd)
            pair_slot[p] = (su, sd)

        for p in range(NSLOT):
            load_pair(p)
        N = EBW

        def n_p2(j):
            hq, hq3 = hqb[j % 2], hqb[j % 2].v3(8)
            norm_stats(xres3[j], 8, N, 1024.0, xc[j], sq, lnb, rsx)
            for c in range(8):
                STT(hq3[:, c, :], xres3[j][:, c, :], vcol(C_GX + c), rsx.ap, ALU.mult, ALU.mult, [xc[j][c], vecs, rsx], [], pw=[hq])

        def q_p2(j, ocs):
            hq, hq3 = hqb[j % 2], hqb[j % 2].v3(8)
            qx, qx3 = qxb[j % 2], qxb[j % 2].v3(8)
            for oc in ocs:
                pb, ph, _ = nextps()
                for kc in range(8):
                    MM(ph[:, 0:N], w_xq3[:, kc, oc * 128:(oc + 1) * 128], hq3[:, kc, :], kc == 0, kc == 7, [w_xq, hq], [pb])
                EVAC(qx3[:, oc, :], ph[:, 0:N], [pb], [], pw=[qx])

        def h_p2(j):
            ox, ox3 = hqb[j % 2], hqb[j % 2].v3(8)
            qx, qx3 = qxb[j % 2], qxb[j % 2].v3(8)

            def s_stage(hd):
                pt = pTx[hd % 2]
                pt3 = pt.v3(2)
                for mt in range(2):
                    pb, ph, _ = nextps()
                    for dc in range(2):
                        MM(ph[:, 0:N], kmT3[:, 2 * hd + dc, mt * 128:(mt + 1) * 128], qx3[:, 2 * hd + dc, :], dc == 0, dc == 1,
                           [kmT, qx], [pb])
                    ACT(pt3[:, mt, :], ph[:, 0:N], AF.Exp, [pb], [pt], scale=1.0 / 16.0)

            def r_stage(hd):
                pt = pTx[hd % 2]
                pt3 = pt.v3(2)
                pbd, phd, _ = nextps()
                for mt in range(2):
                    MM(phd[:, 0:N], ones.ap, pt3[:, mt, :], mt == 0, mt == 1, [ones, pt], [pbd])
                ACT(rdx.ap, phd[:, 0:N], AF.Ln, [pbd], [rdx])
                ACT(rdx.ap, rdx.ap, AF.Exp, [rdx], [rdx], scale=-1.0)
                for dc in range(2):
                    pb, ph, _ = nextps()
                    for mt in range(2):
                        MM(ph[:, 0:N], vm3[:, mt, (2 * hd + dc) * 128:(2 * hd + dc + 1) * 128], pt3[:, mt, :], mt == 0, mt == 1,
                           [vm, pt], [pb])
                    TT(ox3[:, 2 * hd + dc, :], ph[:, 0:N], rdx.ap, ALU.mult, [pb, rdx], [], pw=[ox])

            def fill(i):
                if j + 1 < NEB:
                    q_p2(j + 1, (2 * i, 2 * i + 1))

            s_stage(0)
            s_stage(1)
            r_stage(0)
            s_stage(2)
            fill(0)
            r_stage(1)
            s_stage(3)
            fill(1)
            r_stage(2)
            fill(2)
            r_stage(3)
            fill(3)

        def o_p2(j):
            ox, ox3 = hqb[j % 2], hqb[j % 2].v3(8)
            for oc in range(8):
                pb, ph, _ = nextps()
                for kc in range(8):
                    MM(ph[:, 0:N], w_xo3[:, kc, oc * 128:(oc + 1) * 128], ox3[:, kc, :], kc == 0, kc == 7, [w_xo, ox], [pb])
                TT(xres3[j][:, oc, :], ph[:, 0:N], xres3[j][:, oc, :], ALU.add, [pb, xc[j][oc]], [xc[j][oc]])

        n_p2(0)
        q_p2(0, range(8))
        for j in range(NEB):
            if j + 1 < NEB:
                n_p2(j + 1)
            h_p2(j)
            o_p2(j)

        CK(5)
        A.release(hqb[0], hqb[1], qxb[0], qxb[1], pTx[0], pTx[1], rsx, rdx, w_xq, w_xo, kmT, vm)

        hf = A.alloc("hf", 8 * (NFB * FBW), BF16)
        hf3 = hf.v3(8)
        rsf = A.alloc("rsf", EBW, F32)
        xfl = [b_.ap for b_ in xres]

        def xcols(c0, n):
            out = []
            c = c0
            while c < c0 + n:
                j = c // EBW
                l0 = c - j * EBW
                cnt = min(EBW - l0, c0 + n - c)
                out.append((j, l0, cnt, c - c0))
                c += cnt
            return out

        def hf_piece(j):
            lo = max(F0, j * EBW)
            n = (j + 1) * EBW - lo
            l0 = lo - j * EBW
            src = xres3[j][:, :, l0:l0 + n]
            norm_stats(src, 8, n, 1024.0, xc[j], sq, lnb, rsf)
            for c in range(8):
                STT(hf3[:, c, lo - F0:lo - F0 + n], xres3[j][:, c, l0:l0 + n], vcol(C_GFFN + c), rsf.ap[:, 0:n], ALU.mult, ALU.mult,
                    [xc[j][c], vecs, rsf], [], pw=[hfp[j]])

        hfp = [Buf(f"hfp{j}", None) for j in range(NEB)]
        hf_piece(0)

        TS(hf3[:, :, 0:2], hf3[:, :, 0:2], vcol(C_FLAG), ALU.mult, [hfp[0], vecs], [hfp[0]], s2=0.0, op1=ALU.add)
        ost = A.alloc("ost", 8 * EBW, F32)
        rsz = A.alloc("rsz", EBW, F32)
        tg = [A.alloc(f"tg{i}", EBW, F32) for i in range(2)]
        tv = [A.alloc(f"tv{i}", EBW, F32) for i in range(2)]
        sg = [A.alloc(f"sg{i}", EBW, F32) for i in range(2)]
        actb = [A.alloc(f"actb{i}", EBW, BF16) for i in range(10)]
        xtmp = [A.alloc(f"xtmp{i}", EBW, F32) for i in range(3)]

        def fblock(j):
            lo = max(32, j * EBW)
            return lo - F0, (j + 1) * EBW - lo

        def ffn_up(k, pairs, j):
            o, n = fblock(j)
            acts = []
            pend_mult = None
            for pi, p in enumerate(pairs):
                su, sd = pair_slot[p]
                hs = []
                for half, fc in ((0, p), (1, 22 + p)):
                    pb, ph, _ = nextps()
                    for kc in range(8):
                        MM(ph[:, 0:n + 2], su[half].v3(8)[:, kc, :], hf3[:, kc, o - 2:o + n], kc == 0, kc == 7,
                           [su[half], hf] + hfp[max(0, j - 1):j + 1], [pb])
                    t = (tg if half == 0 else tv)[pi % 2]
                    wc = C_WFC + fc * 3
                    ACT(t.ap[:, 0:n], ph[:, 0:n], AF.Identity, [pb, vecs], [t], scale=vcol(wc), bias=vcol(C_BFC + fc))
                    hs.append((pb, ph, t, wc))
                def stt(h, tap_):
                    pb, ph, t, wc = hs[h]
                    STT(t.ap[:, 0:n], ph[:, tap_:n + tap_], vcol(wc + tap_), t.ap[:, 0:n], ALU.mult, ALU.add, [pb, vecs, t], [t])
                stt(0, 1)
                stt(1, 1)
                stt(0, 2)
                s_ = sg[pi % 2]
                ACT(s_.ap[:, 0:n], hs[0][2].ap[:, 0:n], AF.Silu, [hs[0][2]], [s_])
                if pend_mult is not None:
                    pend_mult()
                stt(1, 2)
                ab = actb[(k % 2) * 5 + pi]

                def mult(s_=s_, tv_=hs[1][2], ab=ab):
                    TT(ab.ap[:, 0:n], s_.ap[:, 0:n], tv_.ap[:, 0:n], ALU.mult, [s_, tv_], [ab])
                pend_mult = mult
                acts.append(ab)
            pend_mult()
            return acts

        xflip = [0]

        def ffn_down(pairs, j, acts, pool_all=False):
            o, n = fblock(j)
            for oc in range(8):
                pb, ph, _ = nextps()
                for pi, p in enumerate(pairs):
                    su, sd = pair_slot[p]
                    MM(ph[:, 0:n], sd.ap[:, oc * 128:(oc + 1) * 128], acts[pi].ap[:, 0:n], pi == 0, pi == len(pairs) - 1,
                       [sd, acts[pi]], [pb])
                if oc % 2 == 0 and not pool_all:
                    for (jb, l0, cnt, do) in xcols(F0 + o, n):
                        TT(xres3[jb][:, oc, l0:l0 + cnt], ph[:, do:do + cnt], xres3[jb][:, oc, l0:l0 + cnt], ALU.add,
                           [pb, xc[jb][oc]], [xc[jb][oc]])
                else:
                    xt = xtmp[xflip[0] % 3]
                    xflip[0] += 1
                    ACT(xt.ap[:, 0:n], ph[:, 0:n], AF.Copy, [pb], [xt])
                    for (jb, l0, cnt, do) in xcols(F0 + o, n):
                        TT(xres3[jb][:, oc, l0:l0 + cnt], xt.ap[:, do:do + cnt], xres3[jb][:, oc, l0:l0 + cnt], ALU.add,
                           [xt, xc[jb][oc]], [xc[jb][oc]], eng="pool")

        def final_block(j):
            lo = max(32, j * EBW)
            n = (j + 1) * EBW - lo
            l0 = lo - j * EBW
            norm_stats(xres3[j][:, :, l0:l0 + n], 8, n, 1024.0, xc[j], sq, lnb, rsz)
            o3 = ost.ap[:, 0:8 * n].rearrange("p (a b) -> p a b", a=8)
            for c in range(8):
                STT(o3[:, c, :], xres3[j][:, c, l0:l0 + n], vcol(C_GFIN + c), rsz.ap[:, 0:n], ALU.mult, ALU.mult,
                    [xc[j][c], vecs, rsz], [], pw=[ost])
            DMA("sp", yT.rearrange("(c p) t -> p c t", p=128)[:, :, lo - 32:lo - 32 + n], o3, [ost], [], outsem[j % 2])

        steps = []
        p0 = 0
        for gi, gsz in enumerate(GROUPS):
            for j in range(NFB):
                steps.append((list(range(p0, p0 + gsz)), j, j == NFB - 1, gi == len(GROUPS) - 1))
            p0 += gsz
        nl = [NSLOT]
        done_pairs = [0]

        def after_down(pairs, last):
            if not last:
                return
            done_pairs[0] = pairs[-1] + 1
            while nl[0] < NPAIR and nl[0] - done_pairs[0] < NSLOT:
                load_pair(nl[0])
                nl[0] += 1

        prev = None
        for k, (pairs, j, last, lastg) in enumerate(steps):
            if k + 1 < NEB:
                hf_piece(k + 1)
            acts = ffn_up(k, pairs, j)
            if prev is not None:
                ffn_down(prev[0], prev[1], prev[4], pool_all=(k - 1 < NFB))
                after_down(prev[0], prev[2])
                if prev[3] and prev[1] >= 1:
                    final_block(prev[1] - 1)
            prev = (pairs, j, last, lastg, acts)
        ffn_down(prev[0], prev[1], prev[4])
        final_block(NEB - 2)
        final_block(NEB - 1)

        CK(6)

    try:
        record()
    except _Stop:
        pass

    S.finalize()
    sems = {}
    for e in S.ENGS:
        for ep in range(S.nepoch[e]):
            sems[(e, ep)] = stack.enter_context(nc.semaphore(f"s_{e}{ep}"))
    for i, b in enumerate(S.dma_bufs):
        sems[("d", id(b))] = stack.enter_context(nc.semaphore(f"d{i}"))
    block = stack.enter_context(nc.Block())

    @block.tensor
    def _(t):
        S.emit("pe", t, sems)

    @block.scalar
    def _(a):
        S.emit("act", a, sems)

    @block.vector
    def _(v):
        S.emit("dve", v, sems)

    @block.gpsimd
    def _(g):
        S.emit("pool", g, sems)

    @block.sync
    def _(sy):
        S.emit("sp", sy, sems, final_waits=outsem)

    stack.close()
    return nc, A.peak, {e: len(S.ops[e]) for e in S.ENGS}


def _t5_bucket_np(dist):
    dist = np.asarray(dist)
    d = np.maximum(dist, 1).astype(np.float32)
    large = 16 + (np.log(d / np.float32(16)) / np.float32(math.log(2048 / 16)) * np.float32(16)).astype(np.int32)
    large = np.minimum(large, 31)
    return np.where(dist < 16, dist, large)


def _bias_tables(rel_bias):
    p = np.arange(128)[:, None]
    c = np.arange(384)[None, :]
    delta = c - p - 128
    valid = (delta >= 0) & (delta <= 128)
    out = np.empty((128, 8, 3, 384), np.float32)
    for di, d in enumerate((1, 4, 16)):
        bucket = _t5_bucket_np(np.clip(delta, 0, 128) * d)
        for h in range(8):
            out[:, h, di, :] = np.where(valid, rel_bias[h][bucket], np.float32(-30000.0))
    return np.ascontiguousarray(out.reshape(128, 24 * 384))


def _chunkcols(v):
    return np.asarray(v, np.float32).reshape(-1, 128).T


_CACHE = {}


def make_in_maps(x, mem, rel_bias, g_mix, w_in, w_short_conv, g_attn_out, g_conv_out, w_out,
                 g_xattn, g_mem, w_xq, w_xk, w_xv, w_xo, g_ffn, w_up, w_ffn_conv, b_ffn_conv,
                 w_down, g_final, cores=range(8)):
    f = lambda a: np.ascontiguousarray(np.asarray(a, np.float32))
    x, mem = f(x), f(mem)
    rel_bias = f(rel_bias)
    vecs = np.zeros((128, NV), np.float32)
    vecs[:, C_GMIX:C_GMIX + 8] = _chunkcols(g_mix[0])
    vecs[:, C_GX:C_GX + 8] = _chunkcols(g_xattn[0])
    vecs[:, C_GMEM:C_GMEM + 8] = _chunkcols(g_mem[0])
    vecs[:, C_GFFN:C_GFFN + 8] = _chunkcols(g_ffn[0])
    vecs[:, C_GFIN:C_GFIN + 8] = _chunkcols(g_final)
    vecs[:, C_GATT:C_GATT + 4] = _chunkcols(g_attn_out[0])
    vecs[:, C_GCONV:C_GCONV + 4] = _chunkcols(g_conv_out[0])
    wsc = f(w_short_conv)[0]
    vecs[:, C_WSC:C_WSC + 12] = wsc.reshape(3, 4, 128).transpose(2, 1, 0).reshape(128, 12)
    wfc = f(w_ffn_conv)[0]
    vecs[:, C_WFC:C_WFC + 132] = wfc.reshape(3, 44, 128).transpose(2, 1, 0).reshape(128, 132)
    vecs[:, C_BFC:C_BFC + 44] = _chunkcols(f(b_ffn_conv)[0])
    vecs[:, C_EPS] = EPS
    vecs[:, C_TINY] = 1e-18
    tbh = _bias_tables(rel_bias)
    ident = np.eye(128, dtype=np.float32)
    shared = dict(tbh=tbh, ident=ident, w_in=f(w_in)[0], w_out=f(w_out)[0], w_xq=f(w_xq)[0], w_xk=f(w_xk)[0],
                  w_xv=f(w_xv)[0], w_xo=f(w_xo)[0], w_up=f(w_up)[0], w_down=f(w_down)[0])
    in_maps = []
    for core in cores:
        b, half = core // 2, core % 2
        xTl = np.zeros((1024, 4096), np.float32)
        if half == 1:
            xTl[:, :] = x[b].T
        else:
            xTl[:, 2048:] = x[b, 0:2048].T
        v = vecs.copy()
        v[:, C_FLAG] = float(half)
        m = dict(shared)
        m.update(xT=xTl, memT=np.ascontiguousarray(mem[b].T), vecs=v)
        in_maps.append(m)
    return in_maps


def kernel(**inputs):
    if "nc" not in _CACHE:
        _CACHE["nc"] = build_program()[0]
    nc = _CACHE["nc"]
    in_maps = make_in_maps(**inputs)
    res = run_bass_kernel_spmd(nc, in_maps, core_ids=list(range(8)))
    out = np.empty((4, 4096, 1024), np.float32)
    for core in range(8):
        b, half = core // 2, core % 2
        out[b, half * 2048:(half + 1) * 2048, :] = res.results[core]["yT"].T
    return out
```

```python
import math
from contextlib import ExitStack
import numpy as np
import concourse.bass as bass
import concourse.mybir as mybir
from concourse.bass_utils import run_bass_kernel_spmd

F32, BF16 = mybir.dt.float32, mybir.dt.bfloat16
AF = mybir.ActivationFunctionType
ALU = mybir.AluOpType

P = 128
E0, NE, EBW, NEB = 2016, 2080, 416, 5
CBW, NCB = 504, 4
F0, FBW, NFB = 30, 410, 5
NPAIR = 22
GROUPS = [5, 4, 5, 4, 4]
NSLOT = 9
EPS = 1e-6
NV = 240
C_GMIX, C_GX, C_GMEM, C_GFFN, C_GFIN, C_GATT, C_GCONV, C_WSC, C_WFC, C_BFC, C_FLAG, C_EPS, C_TINY = \
    0, 8, 16, 24, 32, 40, 44, 48, 60, 192, 236, 237, 238
ARENA_BYTES = 211968
SAME_ENG_SYNC = True
SEM_EPOCH = 30000


class Buf:
    __slots__ = ("name", "ap", "lastw", "readers", "sem", "dcount", "start", "end", "pw")

    def __init__(self, name, ap, start=0, end=0):
        self.name, self.ap, self.start, self.end = name, ap, start, end
        self.lastw, self.readers, self.sem, self.dcount = None, [], None, 0
        self.pw = []

    def v3(self, a):
        return self.ap.rearrange("p (a b) -> p a b", a=a)


class Op:
    __slots__ = ("eng", "fn", "deps", "sig", "val", "dma_buf", "dma_val")


class Sched:
    ENGS = ("pe", "act", "dve", "pool", "sp")

    def __init__(self):
        self.ops = {e: [] for e in self.ENGS}
        self.dma_bufs = []

    def op(self, eng, fn, reads=(), writes=(), dma=None, pwrites=()):
        o = Op()
        o.eng, o.fn, o.sig, o.val, o.dma_buf, o.dma_val = eng, fn, False, 0, None, 0
        deps = set()
        for b in reads:
            if b.lastw is not None:
                deps.add(b.lastw)
            deps.update(b.pw)
        for b in writes:
            if b.lastw is not None:
                deps.add(b.lastw)
            deps.update(b.pw)
            deps.update(b.readers)
        for b in pwrites:
            if b.lastw is not None:
                deps.add(b.lastw)
            deps.update(b.readers)
        keep = []
        for d in deps:
            if d.dma_buf is None and d.eng == eng and (eng == "pe" or not SAME_ENG_SYNC):
                continue
            keep.append(d)
            d.sig = True
        o.deps = keep
        for b in reads:
            b.readers.append(o)
        for b in writes:
            b.lastw = o
            b.readers = []
            b.pw = []
        for b in pwrites:
            b.pw.append(o)
        if dma is not None:
            if dma.dcount == 0:
                self.dma_bufs.append(dma)
            dma.dcount += 1
            o.dma_buf, o.dma_val = dma, 16 * dma.dcount
        self.ops[eng].append(o)
        return o

    def finalize(self):
        self.nepoch = {}
        for e in self.ENGS:
            n = 0
            for o in self.ops[e]:
                if o.dma_buf is None and o.sig:
                    n += 1
                    o.val = n
            self.nepoch[e] = max(1, (n + SEM_EPOCH - 1) // SEM_EPOCH)

    def sigpair(self, o):
        if o.dma_buf is not None:
            return ("d", id(o.dma_buf)), o.dma_val
        ep = (o.val - 1) // SEM_EPOCH
        return (o.eng, ep), o.val - ep * SEM_EPOCH

    def emit(self, eng, engobj, sems, final_waits=()):
        waited = {}
        for o in self.ops[eng]:
            need = {}
            for d in o.deps:
                k, v = self.sigpair(d)
                if v > need.get(k, 0):
                    need[k] = v
            for k, v in need.items():
                if waited.get(k, 0) < v:
                    engobj.wait_ge(sems[k], v)
                    waited[k] = v
            if o.dma_buf is not None and o.dma_val > 16:
                k = ("d", id(o.dma_buf))
                if waited.get(k, 0) < o.dma_val - 16:
                    engobj.wait_ge(sems[k], o.dma_val - 16)
                    waited[k] = o.dma_val - 16
            inst = o.fn(engobj)
            if o.dma_buf is not None:
                inst.then_inc(sems[("d", id(o.dma_buf))], 16)
            elif o.sig:
                k, _ = self.sigpair(o)
                inst.then_inc(sems[k], 1)
        for b in final_waits:
            if b.dcount == 0:
                continue
            engobj.wait_ge(sems[("d", id(b))], 16 * b.dcount)


class Arena:
    def __init__(self, handle, nbytes):
        self.h32 = handle
        self.hbf = handle.bitcast(BF16)
        self.free = [(0, nbytes)]
        self.dead = []
        self.peak = 0
        self.used = 0

    def alloc(self, name, cols, dt):
        esz = 4 if dt == F32 else 2
        nb = (cols * esz + 63) // 64 * 64
        for i, (s, e) in enumerate(self.free):
            if e - s >= nb:
                self.free[i] = (s + nb, e)
                if self.free[i][0] == self.free[i][1]:
                    del self.free[i]
                break
        else:
            raise RuntimeError(f"arena OOM for {name} ({nb} B); free={self.free}")
        if dt == F32:
            ap = self.h32[:, s // 4: s // 4 + cols]
        else:
            ap = self.hbf[:, s // 2: s // 2 + cols]
        b = Buf(name, ap, s, s + nb)
        inh = []
        for (ds, de, db) in self.dead:
            if ds < b.end and b.start < de:
                if db.lastw is not None:
                    inh.append(db.lastw)
                inh.extend(db.pw)
                inh.extend(db.readers)
        b.readers = inh
        self.used += nb
        self.peak = max(self.peak, self.used)
        return b

    def release(self, *bufs):
        for b in bufs:
            self.used -= b.end - b.start
            self.dead.append((b.start, b.end, b))
            self.free.append((b.start, b.end))
        self.free.sort()
        m = []
        for s, e in self.free:
            if m and m[-1][1] == s:
                m[-1] = (m[-1][0], e)
            else:
                m.append((s, e))
        self.free = m


def cols(start, step, count):
    return slice(start, start + step * (count - 1) + 1, step)


class _Stop(Exception):
    pass


def build_program(stop_after=99):
    nc = bass.Bass("TRN2", target_bir_lowering=False)

    def din(name, shape):
        return nc.dram_tensor(name, shape, F32, kind="ExternalInput").ap()

    xT = din("xT", [1024, 4096])
    memT = din("memT", [1024, 256])
    tbh = din("tbh", [128, 24 * 384])
    vecs_d = din("vecs", [128, NV])
    ident_d = din("ident", [128, 128])
    w_in_d = din("w_in", [1024, 3072])
    w_out_d = din("w_out", [1024, 1024])
    w_xq_d = din("w_xq", [1024, 1024])
    w_xk_d = din("w_xk", [1024, 1024])
    w_xv_d = din("w_xv", [1024, 1024])
    w_xo_d = din("w_xo", [1024, 1024])
    w_up_d = din("w_up", [1024, 5632])
    w_down_d = din("w_down", [2816, 1024])
    yT = nc.dram_tensor("yT", [1024, 2048], F32, kind="ExternalOutput").ap()

    def wview(w):
        return w.rearrange("(kc p) n -> p kc n", p=128)

    S = Sched()
    stack = ExitStack()
    arena_h = stack.enter_context(nc.sbuf_tensor("arena", [128, ARENA_BYTES // 4], F32))
    A = Arena(arena_h, ARENA_BYTES)
    psall = stack.enter_context(nc.psum_tensor("psall", [128, 4096], F32))
    psall_bf = psall.bitcast(BF16)

    class PSB:
        def __init__(self, h, base):
            self.h, self.base = h, base

        def __getitem__(self, idx):
            p, c = idx
            return self.h[p, slice(self.base + (c.start or 0), self.base + c.stop, c.step)]

    PS = [(Buf(f"ps{i}", None), PSB(psall, i * 512), PSB(psall_bf, i * 1024)) for i in range(8)]
    psi = [0]

    def nextps():
        r = PS[psi[0] % 8]
        psi[0] += 1
        return r

    outsem = [Buf("outsem0", None), Buf("outsem1", None)]

    def MM(out, lhsT, rhs, start, stop, rd, wr):
        return S.op("pe", lambda e: e.matmul(out, lhsT=lhsT, rhs=rhs, start=start, stop=stop), rd, wr)

    def TR(out, in_, ident, rd, wr):
        return S.op("pe", lambda e: e.transpose(out, in_, ident), rd, wr)

    def ACT(out, in_, func, rd, wr, scale=None, bias=None, pw=()):
        kw = {}
        if scale is not None:
            kw["scale"] = scale
        if bias is not None:
            kw["bias"] = bias
        return S.op("act", lambda e: e.activation(out=out, in_=in_, func=func, **kw), rd, wr, pwrites=pw)

    def TT(out, in0, in1, op, rd, wr, eng="dve", pw=()):
        return S.op(eng, lambda e: e.tensor_tensor(out=out, in0=in0, in1=in1, op=op), rd, wr, pwrites=pw)

    def TS(out, in0, s1, op0, rd, wr, s2=None, op1=None, eng="dve"):
        if op1 is None:
            return S.op(eng, lambda e: e.tensor_scalar(out=out, in0=in0, scalar1=s1, scalar2=None, op0=op0), rd, wr)
        return S.op(eng, lambda e: e.tensor_scalar(out=out, in0=in0, scalar1=s1, scalar2=s2, op0=op0, op1=op1), rd, wr)

    def STT(out, in0, sc, in1, op0, op1, rd, wr, pw=()):
        return S.op("dve", lambda e: e.scalar_tensor_tensor(out=out, in0=in0, scalar=sc, in1=in1, op0=op0, op1=op1), rd, wr,
                    pwrites=pw)

    def CP(out, in_, rd, wr, eng="dve", pw=()):
        if eng == "act":
            return ACT(out, in_, AF.Copy, rd, wr, pw=pw)
        return S.op(eng, lambda e: e.tensor_copy(out=out, in_=in_), rd, wr, pwrites=pw)

    def MSET(ap, val, wr, eng="dve"):
        return S.op(eng, lambda e: e.memset(ap, val), (), wr)

    def DMA(eng, out, in_, rd, wr, dmabuf):
        return S.op(eng, lambda e: e.dma_start(out=out, in_=in_), rd, wr, dma=dmabuf)

    evac_flip = [0]

    def EVAC(out, in_, rd, wr, pw=()):
        evac_flip[0] ^= 1
        return CP(out, in_, rd, wr, eng="act" if evac_flip[0] else "dve", pw=pw)

    def CK(n):
        if n >= stop_after:
            raise _Stop()

    def record():
        vecs = A.alloc("vecs", NV, F32)
        ones = A.alloc("ones", 128, BF16)
        ident = A.alloc("ident", 128, BF16)
        DMA("sp", vecs.ap, vecs_d[:, :], (), [vecs], vecs)
        DMA("pool", ident.ap, ident_d[:, :], (), [ident], ident)
        MSET(ones.ap, 1.0, [ones])

        def vcol(c):
            return vecs.ap[:, c:c + 1]

        def norm_stats(src_ap3, nch, N, dim, srcbufs, sq, lnb, rstd):
            sqv = sq.ap[:, 0:nch * N].rearrange("p (a b) -> p a b", a=nch)
            ACT(sqv, src_ap3, AF.Square, srcbufs, [sq])
            pb, ph, _ = nextps()
            for c in range(nch):
                MM(ph[:, 0:N], ones.ap, sqv[:, c, :], c == 0, c == nch - 1, [ones, sq], [pb])
            ACT(lnb.ap[:, 0:N], ph[:, 0:N], AF.Ln, [pb, vecs], [lnb], scale=1.0 / dim, bias=vcol(C_EPS))
            ACT(rstd.ap[:, 0:N], lnb.ap[:, 0:N], AF.Exp, [lnb], [rstd], scale=-0.5)

        w_in = A.alloc("w_in", 8 * 3072, BF16)
        w_in3 = w_in.v3(8)
        w_in_g = {}

        def load_w_in_group(g, after=()):
            gb = Buf(f"w_in_g{g}", None)
            DMA("pool", w_in3[:, :, g * 512:(g + 1) * 512], wview(w_in_d)[:, :, g * 512:(g + 1) * 512], after, [gb], gb)
            w_in_g[g] = gb

        prev_g = None
        for g_ in (1, 2, 0, 3, 4, 5):
            load_w_in_group(g_, [w_in_g[prev_g]] if prev_g is not None else ())
            prev_g = g_
        kT = A.alloc("kT", 4 * 4096, BF16)
        vT = A.alloc("vT", 4 * 4096, BF16)
        qT = A.alloc("qT", 4 * NE, BF16)
        convT = A.alloc("convT", 4 * NE, BF16)
        kT3, vT3, qT3, convT3 = kT.v3(4), vT.v3(4), qT.v3(4), convT.v3(4)
        xs = A.alloc("xs", 8 * CBW, F32)
        sq = A.alloc("sq", 8 * CBW, BF16)
        hN = [A.alloc(f"hN{i}", 8 * CBW, BF16) for i in range(2)]
        lnb = A.alloc("lnb", CBW, F32)
        rstd = [A.alloc(f"rstd{i}", CBW, F32) for i in range(2)]
        c_sb = [A.alloc(f"c_sb{i}", EBW, F32) for i in range(2)]
        ub = [A.alloc(f"ub{i}", EBW + 2, F32) for i in range(4)]
        t1 = [A.alloc(f"t1_{i}", EBW, F32) for i in range(2)]

        cblk = [(j * CBW, CBW, False) for j in range(NCB)]
        eblk = [(E0 + j * EBW, EBW, True) for j in range(NEB)]
        blocks = []
        for j in range(NEB):
            if j < NCB:
                blocks.append(cblk[j])
            blocks.append(eblk[j])
        first_e = blocks.index(eblk[0])

        def load_norm(j):
            s, N, isE = blocks[j]
            xs3 = xs.ap[:, 0:8 * N].rearrange("p (a b) -> p a b", a=8)
            DMA("sp", xs3, xT.rearrange("(c p) t -> p c t", p=128)[:, :, s:s + N], (), [xs], xs)
            norm_stats(xs3, 8, N, 1024.0, [xs], sq, lnb, rstd[j % 2])
            h3 = hN[j % 2].ap[:, 0:8 * N].rearrange("p (a b) -> p a b", a=8)
            for c in range(8):
                STT(h3[:, c, :], xs3[:, c, :], vcol(C_GMIX + c), rstd[j % 2].ap[:, 0:N], ALU.mult, ALU.mult,
                    [xs, vecs, rstd[j % 2]], [], pw=[hN[j % 2]])

        def proj_group(j, g, oc):
            s, N, isE = blocks[j]
            h3 = hN[j % 2].ap[:, 0:8 * N].rearrange("p (a b) -> p a b", a=8)
            pb, ph, _ = nextps()
            c0 = g * 512 + oc * 128
            for kc in range(8):
                MM(ph[:, 0:N], w_in3[:, kc, c0:c0 + 128], h3[:, kc, :], kc == 0, kc == 7,
                   [w_in_g[g], w_in, hN[j % 2]], [pb])
            return pb, ph

        for i in range(4):
            MSET(ub[i].ap[:, 0:2], 0.0, [ub[i]])

        def proj(j):
            s, N, isE = blocks[j]
            for g, dst, dst3 in ((1, kT, kT3), (2, vT, vT3)):
                for oc in range(4):
                    pb, ph = proj_group(j, g, oc)
                    EVAC(dst3[:, oc, s:s + N], ph[:, 0:N], [pb], [], pw=[dst])
            if not isE:
                return
            e0 = s - E0
            for oc in range(4):
                pb, ph = proj_group(j, 0, oc)
                EVAC(qT3[:, oc, e0:e0 + N], ph[:, 0:N], [pb], [], pw=[qT])
            for oc in range(4):
                pbb, phb = proj_group(j, 3, oc)
                pbc, phc = proj_group(j, 4, oc)
                pbx, phx = proj_group(j, 5, oc)
                cs = c_sb[oc % 2]
                tt = t1[oc % 2]
                u = ub[oc]
                ACT(cs.ap[:, 0:N], phc[:, 0:N], AF.Copy, [pbc], [cs])
                if j > first_e:
                    CP(u.ap[:, 0:2], u.ap[:, N:N + 2], [u], [u])
                TT(u.ap[:, 2:N + 2], phx[:, 0:N], cs.ap[:, 0:N], ALU.mult, [pbx, cs], [u])
                wc = C_WSC + oc * 3
                TS(tt.ap[:, 0:N], u.ap[:, 0:N], vcol(wc), ALU.mult, [u, vecs], [tt])
                STT(tt.ap[:, 0:N], u.ap[:, 1:N + 1], vcol(wc + 1), tt.ap[:, 0:N], ALU.mult, ALU.add, [u, vecs, tt], [tt])
                STT(tt.ap[:, 0:N], u.ap[:, 2:N + 2], vcol(wc + 2), tt.ap[:, 0:N], ALU.mult, ALU.add, [u, vecs, tt], [tt])
                TT(convT3[:, oc, e0:e0 + N], tt.ap[:, 0:N], phb[:, 0:N], ALU.mult, [tt, pbb], [], pw=[convT])

        CK(1)
        load_norm(0)
        proj(0)
        load_norm(1)
        for j in range(1, len(blocks)):
            if j + 1 < len(blocks):
                load_norm(j + 1)
            proj(j)

        CK(2)
        A.release(w_in, xs, sq, hN[0], hN[1], lnb, rstd[0], rstd[1], c_sb[0], c_sb[1], ub[0], ub[1], ub[2], ub[3], t1[0], t1[1])

        Ttp = [A.alloc(f"Ttp{i}", 6 * 384, BF16) for i in range(2)]

        def load_T(hp):
            tb_ = Ttp[hp % 2]
            DMA("pool", tb_.ap, tbh[:, hp * 6 * 384:(hp + 1) * 6 * 384], (), [tb_], tb_)

        def exp_T(hp):
            tb_ = Ttp[hp % 2]
            ACT(tb_.ap, tb_.ap, AF.Exp, [tb_], [tb_])

        load_T(0)
        exp_T(0)
        qm = A.alloc("qm", 2 * NE, BF16)
        qm3 = qm.v3(2)
        MSET(qm3[64:128, 0, :], 0.0, [qm])
        MSET(qm3[0:64, 1, :], 0.0, [qm])
        attnT = A.alloc("attnT", 4 * NE, BF16)
        attnT3 = attnT.v3(4)
        acc = A.alloc("acc", 2 * NE, F32)
        acc3 = acc.v3(2)
        e_sb = [A.alloc(f"e_sb{i}", 512, BF16) for i in range(3)]
        pT = [(A.alloc(f"pTa{i}", 256, BF16), A.alloc(f"pTb{i}", 256, BF16)) for i in range(3)]
        NVS = 8
        vown = [A.alloc(f"vown{i}", 256, BF16) for i in range(NVS)]
        NVC = 4
        vctx = [A.alloc(f"vctx{i}", 256, BF16) for i in range(NVC)]
        rdb = [A.alloc(f"rd{i}", EBW, F32) for i in range(2)]
        wbuf = A.alloc("wbuf", 8 * 1024, BF16)
        wbuf3 = wbuf.v3(8)
        mstage = A.alloc("mstage", 8 * 256, F32)
        msq = A.alloc("msq", 8 * 256, BF16)
        memn = A.alloc("memn", 8 * 256, BF16)
        mln = A.alloc("mln", 256, F32)
        mrs = A.alloc("mrs", 256, F32)
        kmT = A.alloc("kmT", 8 * 256, BF16)
        vm = A.alloc("vm", 2 * 1024, BF16)
        kmT3, vm3, memn3 = kmT.v3(8), vm.v3(2), memn.v3(8)

        for b_ in vown:
            MSET(b_.v3(2)[:, :, 64:128], 1.0, [b_])
        for b_ in vctx:
            TS(b_.v3(2)[:, :, 64:128], ones.ap.rearrange("p (a b) -> p a b", a=2), vcol(C_FLAG), ALU.mult, [ones, vecs], [b_])

        DMA("pool", wbuf3, wview(w_xk_d), (), [wbuf], wbuf)
        ms3 = mstage.v3(8)
        DMA("sp", ms3, memT.rearrange("(c p) t -> p c t", p=128), (), [mstage], mstage)

        def mem_norm():
            norm_stats(ms3, 8, 256, 1024.0, [mstage], msq, mln, mrs)
            for c in range(8):
                STT(memn3[:, c, :], ms3[:, c, :], vcol(C_GMEM + c), mrs.ap[:, 0:256], ALU.mult, ALU.mult,
                    [mstage, vecs, mrs], [], pw=[memn])

        def mem_k():
            for oc in range(8):
                pb, ph, _ = nextps()
                for kc in range(8):
                    MM(ph[:, 0:256], wbuf3[:, kc, oc * 128:(oc + 1) * 128], memn3[:, kc, :], kc == 0, kc == 7, [wbuf, memn], [pb])
                EVAC(kmT3[:, oc, :], ph[:, 0:256], [pb], [], pw=[kmT])
            DMA("pool", wbuf3, wview(w_xv_d), (), [wbuf], wbuf)

        def mem_v():
            for mt in range(2):
                for hf_ in range(2):
                    pb, ph, _ = nextps()
                    for kc in range(8):
                        MM(ph[:, 0:512], memn3[:, kc, mt * 128:(mt + 1) * 128], wbuf3[:, kc, hf_ * 512:(hf_ + 1) * 512],
                           kc == 0, kc == 7, [wbuf, memn], [pb])
                    EVAC(vm3[:, mt, hf_ * 512:(hf_ + 1) * 512], ph[:, 0:512], [pb], [], pw=[vm])

        Trow = Ttp[0].ap.ap[0][0]

        def attention_pair(hp):
            MSET(acc3[:, :, 0:F0], 1.0, [acc])
            if hp + 1 < 4:
                load_T(hp + 1)
            Tt = Ttp[hp % 2]
            CP(qm3[0:64, 0, :], qT3[0:64, hp, :], [qT], [qm])
            CP(qm3[64:128, 1, :], qT3[64:128, hp, :], [qT], [qm])
            vcache = {}
            vrr = {"own": 0, "ctx": 0}

            def get_vtile(d, r, m0):
                key = (d, r, m0)
                if key in vcache:
                    return vcache[key]
                kind = "own" if m0 * d >= 2048 else "ctx"
                slots = vown if kind == "own" else vctx
                sl = slots[vrr[kind] % len(slots)]
                vrr[kind] += 1
                for k_ in [k_ for k_, v_ in vcache.items() if v_ is sl]:
                    del vcache[k_]
                pb, ph, phb = PS[6 + vtn[0] % 2]
                vtn[0] += 1
                TR(phb[:, 0:128], vT3[:, hp, cols(r + d * m0, d, 128)], ident.ap, [vT, ident], [pb])
                CP(sl.v3(2)[:, :, 0:64], phb[:, 0:128].rearrange("p (a b) -> p a b", a=2), [pb], [sl], eng="act")
                vcache[key] = sl
                return sl

            itn = [0]
            vtn = [0]

            def item(d, di, r, n0, W, tiles, first):
                nt = len(tiles)
                it = itn[0]
                itn[0] += 1
                vts = [get_vtile(d, r, m0) for (m0, off) in tiles]
                q0 = r + d * n0 - E0
                pbs, phs, _ = PS[it % 3]
                for ti, (m0, off) in enumerate(tiles):
                    outap = phs[:, ti * 2 * W:(ti + 1) * 2 * W]
                    MM(outap, kT3[:, hp, cols(r + d * m0, d, 128)],
                       qm3[:, :, cols(q0, d, W)], True, True, [kT, qm], [pbs])
                tot = 2 * nt * W
                eb, pb_ = e_sb[it % 3], pT[it % 3]
                ACT(eb.ap[:, 0:tot], phs[:, 0:tot], AF.Exp, [pbs], [eb], scale=0.125)
                for hh in range(2):
                    toff = Tt.ap.offset + (hh * 3 + di) * 384 + tiles[0][1] + 128
                    hw_ = nt * W
                    if nt == 2:
                        tap = bass.AP(tensor=Tt.ap.tensor, offset=toff, ap=[[Trow, 128], [128, 2], [1, W]])
                        ev = eb.ap[:, 0:tot].rearrange("p (b a c) -> p b a c", b=2, a=2)[:, :, hh, :]
                        pv = pb_[hh].ap[:, 0:hw_].rearrange("p (b c) -> p b c", b=2)
                    else:
                        tap = bass.AP(tensor=Tt.ap.tensor, offset=toff, ap=[[Trow, 128], [1, W]])
                        ev = eb.ap[:, hh * W:(hh + 1) * W]
                        pv = pb_[hh].ap[:, 0:hw_]
                    TT(pv, ev, tap, ALU.mult, [eb, Tt], [pb_[hh]], eng="dve" if hh == 0 else "pool")
                return (it, d, q0, W, nt, vts, pb_, first)

            def item_b(ctx):
                it, d, q0, W, nt, vts, pb_, first = ctx
                pbo, pho, _ = PS[3 + it % 3]
                for hh in range(2):
                    for ti in range(nt):
                        MM(pho[:, hh * W:(hh + 1) * W], vts[ti].v3(2)[:, hh, :], pb_[hh].ap[:, ti * W:(ti + 1) * W],
                           ti == 0, ti == nt - 1, [vts[ti], pb_[hh]], [pbo])
                dst = acc3[:, :, cols(q0, d, W)]
                src = pho[:, 0:2 * W].rearrange("p (a c) -> p a c", a=2)
                if first:
                    CP(dst, src, [pbo], [acc])
                else:
                    TT(dst, src, dst, ALU.add, [pbo, acc], [acc])

            pend = []

            def run_item(*a):
                pend.append(item(*a))
                if len(pend) > 2:
                    item_b(pend.pop(0))

            for di, d in enumerate((1, 4, 16)):
                if d == 16 and hp + 1 < 4:
                    exp_T(hp + 1)
                n_own = 2048 // d
                sub = 4096 // d
                for r in range(d):
                    hq_ = [t for t in (2046, 2047) if t % d == r]
                    if hq_:
                        n0 = (hq_[0] - r) // d
                        W = len(hq_)
                        mB = (n0 // 128) * 128
                        tiles = [(mB, n0 - mB)]
                        if mB - 128 >= 0:
                            tiles.append((mB - 128, n0 - mB + 128))
                        run_item(d, di, r, n0, W, tiles, d == 1)
                    for n0 in range(n_own, sub, 128):
                        run_item(d, di, r, n0, 128, [(n0, 0), (n0 - 128, 128)], d == 1)
            while pend:
                item_b(pend.pop(0))
            for hh in range(2):
                ACT(acc3[64:128, hh, :], acc3[64:128, hh, :], AF.Ln, [acc, vecs], [acc], bias=vecs.ap[64:128, C_TINY:C_TINY + 1])
                for cb in range(NEB):
                    c0 = cb * EBW
                    rb = rdb[cb % 2]
                    ACT(rb.ap[0:64, :], acc3[64:128, hh, c0:c0 + EBW], AF.Exp, [acc], [rb], scale=-1.0)
                    TT(attnT3[hh * 64:(hh + 1) * 64, hp, c0:c0 + EBW], acc3[0:64, hh, c0:c0 + EBW], rb.ap[0:64, :], ALU.mult,
                       [acc, rb], [], pw=[attnT])

        attention_pair(0)
        mem_norm()
        mem_k()
        attention_pair(1)
        mem_v()
        attention_pair(2)
        attention_pair(3)

        CK(3)
        A.release(kT, vT, qT, Ttp[0], Ttp[1], qm, acc, *e_sb, *[b_ for pr in pT for b_ in pr], rdb[0], rdb[1], wbuf, mstage, msq, memn, mln, mrs,
                  *vown, *vctx)

        w_out = A.alloc("w_out", 8 * 1024, BF16)
        w_out3 = w_out.v3(8)
        DMA("pool", w_out3, wview(w_out_d), (), [w_out], w_out)
        xres = [A.alloc(f"xres{j}", 8 * EBW, F32) for j in range(NEB)]
        xres3 = [b_.v3(8) for b_ in xres]
        xc = [[Buf(f"xc{j}_{c}", None) for c in range(8)] for j in range(NEB)]
        for j in range(NEB):
            s = E0 + j * EBW
            DMA("sp", xres3[j], xT.rearrange("(c p) t -> p c t", p=128)[:, :, s:s + EBW], (), [xres[j]] + xc[j], xres[j])
        w_xq = A.alloc("w_xq", 8 * 1024, BF16)
        w_xo = A.alloc("w_xo", 8 * 1024, BF16)
        w_xq3, w_xo3 = w_xq.v3(8), w_xo.v3(8)
        DMA("pool", w_xq3, wview(w_xq_d), (), [w_xq], w_xq)
        DMA("pool", w_xo3, wview(w_xo_d), (), [w_xo], w_xo)
        mixed = [A.alloc(f"mixed{i}", 8 * EBW, BF16) for i in range(2)]
        sq = A.alloc("sq2", 8 * EBW, BF16)
        lnb = A.alloc("lnb2", EBW, F32)
        rsa = A.alloc("rsa", EBW, F32)
        rsc = A.alloc("rsc", EBW, F32)

        def nm_1c(j):
            c0 = j * EBW
            mx3 = mixed[j % 2].v3(8)
            norm_stats(attnT3[:, :, c0:c0 + EBW], 4, EBW, 512.0, [attnT], sq, lnb, rsa)
            for c in range(4):
                STT(mx3[:, c, :], attnT3[:, c, c0:c0 + EBW], vcol(C_GATT + c), rsa.ap, ALU.mult, ALU.mult,
                    [attnT, vecs, rsa], [], pw=[mixed[j % 2]])
            norm_stats(convT3[:, :, c0:c0 + EBW], 4, EBW, 512.0, [convT], sq, lnb, rsc)
            for c in range(4):
                STT(mx3[:, 4 + c, :], convT3[:, c, c0:c0 + EBW], vcol(C_GCONV + c), rsc.ap, ALU.mult, ALU.mult,
                    [convT, vecs, rsc], [], pw=[mixed[j % 2]])

        def op_1c(j):
            mx3 = mixed[j % 2].v3(8)
            for oc in range(8):
                pb, ph, _ = nextps()
                for kc in range(8):
                    MM(ph[:, 0:EBW], w_out3[:, kc, oc * 128:(oc + 1) * 128], mx3[:, kc, :], kc == 0, kc == 7,
                       [w_out, mixed[j % 2]], [pb])
                TT(xres3[j][:, oc, :], ph[:, 0:EBW], xres3[j][:, oc, :], ALU.add, [pb, xc[j][oc]], [xc[j][oc]])

        nm_1c(0)
        for j in range(NEB):
            if j + 1 < NEB:
                nm_1c(j + 1)
            op_1c(j)

        CK(4)
        A.release(attnT, convT, w_out, mixed[0], mixed[1], rsa, rsc)

        hqb = [A.alloc(f"hq{i}", 8 * EBW, BF16) for i in range(2)]
        qxb = [A.alloc(f"qx{i}", 8 * EBW, BF16) for i in range(2)]
        pTx = [A.alloc(f"pTx{i}", 2 * EBW, BF16) for i in range(2)]
        rsx = A.alloc("rsx", EBW, F32)
        rdx = A.alloc("rdx", EBW, F32)
        slots = [((A.alloc(f"wupg{i}", 8 * 128, BF16), A.alloc(f"wupv{i}", 8 * 128, BF16)), A.alloc(f"wdn{i}", 1024, BF16))
                 for i in range(NSLOT)]
        pair_slot = {}

        def load_pair(p):
            su, sd = slots[p % NSLOT]
            DMA("pool", su[0].v3(8), wview(w_up_d)[:, :, p * 128:(p + 1) * 128], (), [su[0]], su[0])
            DMA("pool", su[1].v3(8), wview(w_up_d)[:, :, 2816 + p * 128:2816 + (p + 1) * 128], (), [su[1]], su[1])
            DMA("pool", sd.ap, w_down_d[p * 128:(p + 1) * 128, :], (), [sd], sd)
            pair_slot[p] = (su, sd)

        for p in range(NSLOT):
            load_pair(p)
        N = EBW

        def n_p2(j):
            hq, hq3 = hqb[j % 2], hqb[j % 2].v3(8)
            norm_stats(xres3[j], 8, N, 1024.0, xc[j], sq, lnb, rsx)
            for c in range(8):
                STT(hq3[:, c, :], xres3[j][:, c, :], vcol(C_GX + c), rsx.ap, ALU.mult, ALU.mult, [xc[j][c], vecs, rsx], [], pw=[hq])

        def q_p2(j, ocs):
            hq, hq3 = hqb[j % 2], hqb[j % 2].v3(8)
            qx, qx3 = qxb[j % 2], qxb[j % 2].v3(8)
            for oc in ocs:
                pb, ph, _ = nextps()
                for kc in range(8):
                    MM(ph[:, 0:N], w_xq3[:, kc, oc * 128:(oc + 1) * 128], hq3[:, kc, :], kc == 0, kc == 7, [w_xq, hq], [pb])
                EVAC(qx3[:, oc, :], ph[:, 0:N], [pb], [], pw=[qx])

        def h_p2(j):
            ox, ox3 = hqb[j % 2], hqb[j % 2].v3(8)
            qx, qx3 = qxb[j % 2], qxb[j % 2].v3(8)

            def s_stage(hd):
                pt = pTx[hd % 2]
                pt3 = pt.v3(2)
                for mt in range(2):
                    pb, ph, _ = nextps()
                    for dc in range(2):
                        MM(ph[:, 0:N], kmT3[:, 2 * hd + dc, mt * 128:(mt + 1) * 128], qx3[:, 2 * hd + dc, :], dc == 0, dc == 1,
                           [kmT, qx], [pb])
                    ACT(pt3[:, mt, :], ph[:, 0:N], AF.Exp, [pb], [pt], scale=1.0 / 16.0)

            def r_stage(hd):
                pt = pTx[hd % 2]
                pt3 = pt.v3(2)
                pbd, phd, _ = nextps()
                for mt in range(2):
                    MM(phd[:, 0:N], ones.ap, pt3[:, mt, :], mt == 0, mt == 1, [ones, pt], [pbd])
                ACT(rdx.ap, phd[:, 0:N], AF.Ln, [pbd], [rdx])
                ACT(rdx.ap, rdx.ap, AF.Exp, [rdx], [rdx], scale=-1.0)
                for dc in range(2):
                    pb, ph, _ = nextps()
                    for mt in range(2):
                        MM(ph[:, 0:N], vm3[:, mt, (2 * hd + dc) * 128:(2 * hd + dc + 1) * 128], pt3[:, mt, :], mt == 0, mt == 1,
                           [vm, pt], [pb])
                    TT(ox3[:, 2 * hd + dc, :], ph[:, 0:N], rdx.ap, ALU.mult, [pb, rdx], [], pw=[ox])

            def fill(i):
                if j + 1 < NEB:
                    q_p2(j + 1, (2 * i, 2 * i + 1))

            s_stage(0)
            s_stage(1)
            r_stage(0)
            s_stage(2)
            fill(0)
            r_stage(1)
            s_stage(3)
            fill(1)
            r_stage(2)
            fill(2)
            r_stage(3)
            fill(3)

        def o_p2(j):
            ox, ox3 = hqb[j % 2], hqb[j % 2].v3(8)
            for oc in range(8):
                pb, ph, _ = nextps()
                for kc in range(8):
                    MM(ph[:, 0:N], w_xo3[:, kc, oc * 128:(oc + 1) * 128], ox3[:, kc, :], kc == 0, kc == 7, [w_xo, ox], [pb])
                TT(xres3[j][:, oc, :], ph[:, 0:N], xres3[j][:, oc, :], ALU.add, [pb, xc[j][oc]], [xc[j][oc]])

        n_p2(0)
        q_p2(0, range(8))
        for j in range(NEB):
            if j + 1 < NEB:
                n_p2(j + 1)
            h_p2(j)
            o_p2(j)

        CK(5)
        A.release(hqb[0], hqb[1], qxb[0], qxb[1], pTx[0], pTx[1], rsx, rdx, w_xq, w_xo, kmT, vm)

        hf = A.alloc("hf", 8 * (NFB * FBW), BF16)
        hf3 = hf.v3(8)
        rsf = A.alloc("rsf", EBW, F32)
        xfl = [b_.ap for b_ in xres]

        def xcols(c0, n):
            out = []
            c = c0
            while c < c0 + n:
                j = c // EBW
                l0 = c - j * EBW
                cnt = min(EBW - l0, c0 + n - c)
                out.append((j, l0, cnt, c - c0))
                c += cnt
            return out

        def hf_piece(j):
            lo = max(F0, j * EBW)
            n = (j + 1) * EBW - lo
            l0 = lo - j * EBW
            src = xres3[j][:, :, l0:l0 + n]
            norm_stats(src, 8, n, 1024.0, xc[j], sq, lnb, rsf)
            for c in range(8):
                STT(hf3[:, c, lo - F0:lo - F0 + n], xres3[j][:, c, l0:l0 + n], vcol(C_GFFN + c), rsf.ap[:, 0:n], ALU.mult, ALU.mult,
                    [xc[j][c], vecs, rsf], [], pw=[hfp[j]])

        hfp = [Buf(f"hfp{j}", None) for j in range(NEB)]
        hf_piece(0)

        TS(hf3[:, :, 0:2], hf3[:, :, 0:2], vcol(C_FLAG), ALU.mult, [hfp[0], vecs], [hfp[0]], s2=0.0, op1=ALU.add)
        ost = A.alloc("ost", 8 * EBW, F32)
        rsz = A.alloc("rsz", EBW, F32)
        tg = [A.alloc(f"tg{i}", EBW, F32) for i in range(2)]
        tv = [A.alloc(f"tv{i}", EBW, F32) for i in range(2)]
        sg = [A.alloc(f"sg{i}", EBW, F32) for i in range(2)]
        actb = [A.alloc(f"actb{i}", EBW, BF16) for i in range(10)]
        xtmp = [A.alloc(f"xtmp{i}", EBW, F32) for i in range(3)]

        def fblock(j):
            lo = max(32, j * EBW)
            return lo - F0, (j + 1) * EBW - lo

        def ffn_up(k, pairs, j):
            o, n = fblock(j)
            acts = []
            pend_mult = None
            for pi, p in enumerate(pairs):
                su, sd = pair_slot[p]
                hs = []
                for half, fc in ((0, p), (1, 22 + p)):
                    pb, ph, _ = nextps()
                    for kc in range(8):
                        MM(ph[:, 0:n + 2], su[half].v3(8)[:, kc, :], hf3[:, kc, o - 2:o + n], kc == 0, kc == 7,
                           [su[half], hf] + hfp[max(0, j - 1):j + 1], [pb])
                    t = (tg if half == 0 else tv)[pi % 2]
                    wc = C_WFC + fc * 3
                    ACT(t.ap[:, 0:n], ph[:, 0:n], AF.Identity, [pb, vecs], [t], scale=vcol(wc), bias=vcol(C_BFC + fc))
                    hs.append((pb, ph, t, wc))
                def stt(h, tap_):
                    pb, ph, t, wc = hs[h]
                    STT(t.ap[:, 0:n], ph[:, tap_:n + tap_], vcol(wc + tap_), t.ap[:, 0:n], ALU.mult, ALU.add, [pb, vecs, t], [t])
                stt(0, 1)
                stt(1, 1)
                stt(0, 2)
                s_ = sg[pi % 2]
                ACT(s_.ap[:, 0:n], hs[0][2].ap[:, 0:n], AF.Silu, [hs[0][2]], [s_])
                if pend_mult is not None:
                    pend_mult()
                stt(1, 2)
                ab = actb[(k % 2) * 5 + pi]

                def mult(s_=s_, tv_=hs[1][2], ab=ab):
                    TT(ab.ap[:, 0:n], s_.ap[:, 0:n], tv_.ap[:, 0:n], ALU.mult, [s_, tv_], [ab])
                pend_mult = mult
                acts.append(ab)
            pend_mult()
            return acts

        xflip = [0]

        def ffn_down(pairs, j, acts):
            o, n = fblock(j)
            for oc in range(8):
                pb, ph, _ = nextps()
                for pi, p in enumerate(pairs):
                    su, sd = pair_slot[p]
                    MM(ph[:, 0:n], sd.ap[:, oc * 128:(oc + 1) * 128], acts[pi].ap[:, 0:n], pi == 0, pi == len(pairs) - 1,
                       [sd, acts[pi]], [pb])
                if oc % 2 == 0:
                    for (jb, l0, cnt, do) in xcols(F0 + o, n):
                        TT(xres3[jb][:, oc, l0:l0 + cnt], ph[:, do:do + cnt], xres3[jb][:, oc, l0:l0 + cnt], ALU.add,
                           [pb, xc[jb][oc]], [xc[jb][oc]])
                else:
                    xt = xtmp[xflip[0] % 3]
                    xflip[0] += 1
                    ACT(xt.ap[:, 0:n], ph[:, 0:n], AF.Copy, [pb], [xt])
                    for (jb, l0, cnt, do) in xcols(F0 + o, n):
                        TT(xres3[jb][:, oc, l0:l0 + cnt], xt.ap[:, do:do + cnt], xres3[jb][:, oc, l0:l0 + cnt], ALU.add,
                           [xt, xc[jb][oc]], [xc[jb][oc]], eng="pool")

        def final_block(j):
            lo = max(32, j * EBW)
            n = (j + 1) * EBW - lo
            l0 = lo - j * EBW
            norm_stats(xres3[j][:, :, l0:l0 + n], 8, n, 1024.0, xc[j], sq, lnb, rsz)
            o3 = ost.ap[:, 0:8 * n].rearrange("p (a b) -> p a b", a=8)
            for c in range(8):
                STT(o3[:, c, :], xres3[j][:, c, l0:l0 + n], vcol(C_GFIN + c), rsz.ap[:, 0:n], ALU.mult, ALU.mult,
                    [xc[j][c], vecs, rsz], [], pw=[ost])
            DMA("sp", yT.rearrange("(c p) t -> p c t", p=128)[:, :, lo - 32:lo - 32 + n], o3, [ost], [], outsem[j % 2])

        steps = []
        p0 = 0
        for gi, gsz in enumerate(GROUPS):
            for j in range(NFB):
                steps.append((list(range(p0, p0 + gsz)), j, j == NFB - 1, gi == len(GROUPS) - 1))
            p0 += gsz
        nl = [NSLOT]
        done_pairs = [0]

        def after_down(pairs, last):
            if not last:
                return
            done_pairs[0] = pairs[-1] + 1
            while nl[0] < NPAIR and nl[0] - done_pairs[0] < NSLOT:
                load_pair(nl[0])
                nl[0] += 1

        prev = None
        for k, (pairs, j, last, lastg) in enumerate(steps):
            if k + 1 < NEB:
                hf_piece(k + 1)
            acts = ffn_up(k, pairs, j)
            if prev is not None:
                ffn_down(prev[0], prev[1], prev[4])
                after_down(prev[0], prev[2])
                if prev[3] and prev[1] >= 1:
                    final_block(prev[1] - 1)
            prev = (pairs, j, last, lastg, acts)
        ffn_down(prev[0], prev[1], prev[4])
        final_block(NEB - 2)
        final_block(NEB - 1)

        CK(6)

    try:
        record()
    except _Stop:
        pass

    S.finalize()
    sems = {}
    for e in S.ENGS:
        for ep in range(S.nepoch[e]):
            sems[(e, ep)] = stack.enter_context(nc.semaphore(f"s_{e}{ep}"))
    for i, b in enumerate(S.dma_bufs):
        sems[("d", id(b))] = stack.enter_context(nc.semaphore(f"d{i}"))
    block = stack.enter_context(nc.Block())

    @block.tensor
    def _(t):
        S.emit("pe", t, sems)

    @block.scalar
    def _(a):
        S.emit("act", a, sems)

    @block.vector
    def _(v):
        S.emit("dve", v, sems)

    @block.gpsimd
    def _(g):
        S.emit("pool", g, sems)

    @block.sync
    def _(sy):
        S.emit("sp", sy, sems, final_waits=outsem)

    stack.close()
    return nc, A.peak, {e: len(S.ops[e]) for e in S.ENGS}


def _t5_bucket_np(dist):
    dist = np.asarray(dist)
    d = np.maximum(dist, 1).astype(np.float32)
    large = 16 + (np.log(d / np.float32(16)) / np.float32(math.log(2048 / 16)) * np.float32(16)).astype(np.int32)
    large = np.minimum(large, 31)
    return np.where(dist < 16, dist, large)


def _bias_tables(rel_bias):
    p = np.arange(128)[:, None]
    c = np.arange(384)[None, :]
    delta = c - p - 128
    valid = (delta >= 0) & (delta <= 128)
    out = np.empty((128, 8, 3, 384), np.float32)
    for di, d in enumerate((1, 4, 16)):
        bucket = _t5_bucket_np(np.clip(delta, 0, 128) * d)
        for h in range(8):
            out[:, h, di, :] = np.where(valid, rel_bias[h][bucket], np.float32(-30000.0))
    return np.ascontiguousarray(out.reshape(128, 24 * 384))


def _chunkcols(v):
    return np.asarray(v, np.float32).reshape(-1, 128).T


_CACHE = {}


def make_in_maps(x, mem, rel_bias, g_mix, w_in, w_short_conv, g_attn_out, g_conv_out, w_out,
                 g_xattn, g_mem, w_xq, w_xk, w_xv, w_xo, g_ffn, w_up, w_ffn_conv, b_ffn_conv,
                 w_down, g_final, cores=range(8)):
    f = lambda a: np.ascontiguousarray(np.asarray(a, np.float32))
    x, mem = f(x), f(mem)
    rel_bias = f(rel_bias)
    vecs = np.zeros((128, NV), np.float32)
    vecs[:, C_GMIX:C_GMIX + 8] = _chunkcols(g_mix[0])
    vecs[:, C_GX:C_GX + 8] = _chunkcols(g_xattn[0])
    vecs[:, C_GMEM:C_GMEM + 8] = _chunkcols(g_mem[0])
    vecs[:, C_GFFN:C_GFFN + 8] = _chunkcols(g_ffn[0])
    vecs[:, C_GFIN:C_GFIN + 8] = _chunkcols(g_final)
    vecs[:, C_GATT:C_GATT + 4] = _chunkcols(g_attn_out[0])
    vecs[:, C_GCONV:C_GCONV + 4] = _chunkcols(g_conv_out[0])
    wsc = f(w_short_conv)[0]
    vecs[:, C_WSC:C_WSC + 12] = wsc.reshape(3, 4, 128).transpose(2, 1, 0).reshape(128, 12)
    wfc = f(w_ffn_conv)[0]
    vecs[:, C_WFC:C_WFC + 132] = wfc.reshape(3, 44, 128).transpose(2, 1, 0).reshape(128, 132)
    vecs[:, C_BFC:C_BFC + 44] = _chunkcols(f(b_ffn_conv)[0])
    vecs[:, C_EPS] = EPS
    vecs[:, C_TINY] = 1e-18
    tbh = _bias_tables(rel_bias)
    ident = np.eye(128, dtype=np.float32)
    shared = dict(tbh=tbh, ident=ident, w_in=f(w_in)[0], w_out=f(w_out)[0], w_xq=f(w_xq)[0], w_xk=f(w_xk)[0],
                  w_xv=f(w_xv)[0], w_xo=f(w_xo)[0], w_up=f(w_up)[0], w_down=f(w_down)[0])
    in_maps = []
    for core in cores:
        b, half = core // 2, core % 2
        xTl = np.zeros((1024, 4096), np.float32)
        if half == 1:
            xTl[:, :] = x[b].T
        else:
            xTl[:, 2048:] = x[b, 0:2048].T
        v = vecs.copy()
        v[:, C_FLAG] = float(half)
        m = dict(shared)
        m.update(xT=xTl, memT=np.ascontiguousarray(mem[b].T), vecs=v)
        in_maps.append(m)
    return in_maps


def kernel(**inputs):
    if "nc" not in _CACHE:
        _CACHE["nc"] = build_program()[0]
    nc = _CACHE["nc"]
    in_maps = make_in_maps(**inputs)
    res = run_bass_kernel_spmd(nc, in_maps, core_ids=list(range(8)))
    out = np.empty((4, 4096, 1024), np.float32)
    for core in range(8):
        b, half = core // 2, core % 2
        out[b, half * 2048:(half + 1) * 2048, :] = res.results[core]["yT"].T
    return out
```
